# Optimizing a Trainium2 kernel written in Bass

```python
import jax
import jax.numpy as jnp
from jax import lax
import numpy as np

D_MODEL = 2048
BATCH = 4
SEQ = 4096
DEPTH = 4

D_HGRN = D_MODEL // 2
HGRN_HEAD_DIM = 128
HGRN_HEADS = D_HGRN // HGRN_HEAD_DIM
D_POOL = D_MODEL - D_HGRN
POOL_WINDOWS = (2, 4, 8, 16)
POOL_GROUPS = len(POOL_WINDOWS)
POOL_GROUP_DIM = D_POOL // POOL_GROUPS
D_MIX = D_HGRN + D_POOL
D_IN = 4 * D_HGRN + D_POOL
D_FF = ((8 * D_MODEL // 3 + 255) // 256) * 256
CONV_WIDTH = 3
CHUNK = 64
N_MOD = 6
EPS = 1e-6

kernel_name = 'hymba_style_hgrn2_pool_hybrid'


def rms_norm(x, gain):
    xf = x.astype(jnp.float32)
    y = xf * lax.rsqrt(jnp.mean(xf * xf, axis=-1, keepdims=True) + EPS)
    return (y * gain.astype(jnp.float32)).astype(x.dtype)


def hgrn2_chunked(q, k, v, log_f):
    B, S, H, Dk = q.shape
    Dv = v.shape[-1]
    n = S // CHUNK

    def to_chunks(t):
        return t.reshape(B, n, CHUNK, H, t.shape[-1]).transpose(1, 0, 3, 2, 4)

    qc, kc, vc, gc = (to_chunks(t) for t in (q, k, v, log_f))
    causal = jnp.tril(jnp.ones((CHUNK, CHUNK), dtype=bool))[:, :, None]

    def step(state, inp):
        qn, kn, vn, gn = inp
        b = jnp.cumsum(gn, axis=-2)
        diff = b[:, :, :, None, :] - b[:, :, None, :, :]
        decay = jnp.where(causal, jnp.exp(jnp.where(causal, diff, 0.0)), 0.0)
        scores = jnp.einsum('bhtd,bhsd,bhtsd->bhts', qn, kn, decay)
        o = (jnp.einsum('bhts,bhsv->bhtv', scores, vn)
             + jnp.einsum('bhtd,bhdv->bhtv', qn * jnp.exp(b), state))
        b_last = b[:, :, -1:, :]
        state = (jnp.exp(b_last[:, :, 0, :, None]) * state
                 + jnp.einsum('bhsd,bhsv->bhdv', kn * jnp.exp(b_last - b), vn))
        return state, o

    s0 = jnp.zeros((B, H, Dk, Dv), jnp.float32)
    _, o = lax.scan(step, s0, (qc, kc, vc, gc))
    return o.transpose(1, 0, 3, 2, 4).reshape(B, S, H, Dv)


def hgrn2_branch(q, f, i, g, lb, norm_g):
    B, S, _ = q.shape

    def heads(t):
        return t.astype(jnp.float32).reshape(B, S, HGRN_HEADS, HGRN_HEAD_DIM)

    qh = heads(jax.nn.silu(q)) * (HGRN_HEAD_DIM ** -0.5)
    z = heads(f)
    lbh = lb.astype(jnp.float32).reshape(HGRN_HEADS, HGRN_HEAD_DIM)
    log_f = jax.nn.log_sigmoid(z) + jnp.log1p(lbh * jnp.exp(-z))
    kh = (1.0 - lbh) * jax.nn.sigmoid(-z)
    o = hgrn2_chunked(qh, kh, heads(i), log_f)
    o = rms_norm(o, norm_g).reshape(B, S, D_HGRN)
    return (o * jax.nn.silu(g.astype(jnp.float32))).astype(q.dtype)


def pool_branch(u, pool_w, pool_scale):
    B, S, _ = u.shape
    uf = u.astype(jnp.float32).reshape(B, S, POOL_GROUPS, POOL_GROUP_DIM)
    cs = jnp.pad(jnp.cumsum(uf, axis=1), ((0, 0), (1, 0), (0, 0), (0, 0)))
    counts_base = jnp.arange(1, S + 1, dtype=jnp.float32)
    means = []
    for gi, w in enumerate(POOL_WINDOWS):
        csg = cs[:, :, gi]
        prev = jnp.pad(csg, ((0, 0), (w - 1, 0), (0, 0)))[:, :S]
        count = jnp.minimum(counts_base, float(w))
        means.append((csg[:, 1:] - prev) / count[None, :, None])
    pooled = jnp.stack(means, axis=2) - uf
    y = jnp.einsum('bsgd,gde->bsge', pooled, pool_w.astype(jnp.float32)).reshape(B, S, D_POOL)
    return (y * pool_scale.astype(jnp.float32)).astype(u.dtype)


def conv_glu(h, w_up, conv_w, conv_b, w_down):
    a, v = jnp.split(h @ w_up, 2, axis=-1)
    S = a.shape[1]
    ap = jnp.pad(a, ((0, 0), (CONV_WIDTH - 1, 0), (0, 0)))
    conv = conv_b
    for tap in range(CONV_WIDTH):
        conv = conv + conv_w[tap] * ap[:, tap:tap + S]
    return (jax.nn.silu(conv) * v) @ w_down


def setup_inputs(seed: int = 0):
    key = jax.random.key(seed)
    ks = jax.random.split(key, 17)
    f32 = jnp.float32

    def nrm(k, shape, scale):
        return jax.random.normal(k, shape, f32) * scale

    def gain(k, shape):
        return 1.0 + 0.02 * jax.random.normal(k, shape, f32)

    return {
        'x': nrm(ks[0], (BATCH, SEQ, D_MODEL), 1.0),
        'c': nrm(ks[1], (BATCH, D_MODEL), 1.0),
        'ada_w': nrm(ks[2], (DEPTH, D_MODEL, N_MOD * D_MODEL), 0.5 * D_MODEL ** -0.5),
        'ada_b': nrm(ks[3], (DEPTH, N_MOD * D_MODEL), 0.01),
        'mix_norm_g': gain(ks[4], (DEPTH, D_MODEL)),
        'w_in': nrm(ks[5], (DEPTH, D_MODEL, D_IN), D_MODEL ** -0.5),
        'hgrn_lower_bounds': nrm(ks[6], (DEPTH, D_HGRN), 0.1),
        'hgrn_norm_g': gain(ks[7], (DEPTH, HGRN_HEAD_DIM)),
        'pool_w': nrm(ks[8], (DEPTH, POOL_GROUPS, POOL_GROUP_DIM, POOL_GROUP_DIM), POOL_GROUP_DIM ** -0.5),
        'pool_scale': gain(ks[9], (DEPTH, D_POOL)),
        'w_out': nrm(ks[10], (DEPTH, D_MIX, D_MODEL), D_MIX ** -0.5),
        'ffn_norm_g': gain(ks[11], (DEPTH, D_MODEL)),
        'w_up': nrm(ks[12], (DEPTH, D_MODEL, 2 * D_FF), D_MODEL ** -0.5),
        'conv_w': nrm(ks[13], (DEPTH, CONV_WIDTH, D_FF), CONV_WIDTH ** -0.5),
        'conv_b': nrm(ks[14], (DEPTH, D_FF), 0.01),
        'w_down': nrm(ks[15], (DEPTH, D_FF, D_MODEL), D_FF ** -0.5),
        'final_norm_g': gain(ks[16], (D_MODEL,)),
    }


def reference(x, c, ada_w, ada_b, mix_norm_g, w_in, hgrn_lower_bounds, hgrn_norm_g, pool_w,
              pool_scale, w_out, ffn_norm_g, w_up, conv_w, conv_b, w_down, final_norm_g):
    p = jax.nn.softmax(hgrn_lower_bounds.astype(jnp.float32), axis=0)
    lower_bounds = jnp.clip(jnp.cumsum(p, axis=0) - p[0:1], 0.0, 1.0)
    c_act = jax.nn.silu(c)
    for l in range(DEPTH):
        mod = (c_act @ ada_w[l] + ada_b[l])[:, None, :]
        shift1, scale1, gate1, shift2, scale2, gate2 = jnp.split(mod, N_MOD, axis=-1)

        h = rms_norm(x, mix_norm_g[l]) * (1.0 + scale1) + shift1
        proj = h @ w_in[l]
        q, f, i, g, u = jnp.split(proj, [D_HGRN, 2 * D_HGRN, 3 * D_HGRN, 4 * D_HGRN], axis=-1)
        y_a = hgrn2_branch(q, f, i, g, lower_bounds[l], hgrn_norm_g[l])
        y_b = pool_branch(u, pool_w[l], pool_scale[l])
        x = x + gate1 * (jnp.concatenate([y_a, y_b], axis=-1) @ w_out[l])

        h = rms_norm(x, ffn_norm_g[l]) * (1.0 + scale2) + shift2
        x = x + gate2 * conv_glu(h, w_up[l], conv_w[l], conv_b[l], w_down[l])
    return rms_norm(x, final_norm_g)
```

```python
import numpy as np
from contextlib import ExitStack
import concourse.bass as bass
import concourse.mybir as mybir
from concourse.bass_utils import run_bass_kernel_spmd

F32 = mybir.dt.float32
BF16 = mybir.dt.bfloat16
I32 = mybir.dt.int32
U8 = mybir.dt.uint8
AF = mybir.ActivationFunctionType
ALU = mybir.AluOpType

ENGS = ("pe", "act", "dve", "pool", "sp")


class Op:
    __slots__ = ("eng", "fn", "reads", "writes", "dma", "dkey", "dinc", "idx",
                 "need_inc", "sem", "tick", "waits", "epoch", "grp")

    def __init__(self, eng, fn, reads, writes, dma, dkey, dinc):
        self.eng = eng; self.fn = fn; self.reads = tuple(reads); self.writes = tuple(writes)
        self.dma = dma; self.dkey = dkey; self.dinc = dinc
        self.need_inc = False; self.sem = None; self.tick = 0; self.waits = []; self.epoch = 0; self.grp = None


class Prog:
    def __init__(self, nc, self_sync=True):
        self.nc = nc
        self.ops = []
        self.self_sync = self_sync
        self.cur_epoch = 0

    def epoch(self):
        self.cur_epoch += 1

    def op(self, eng, fn, reads=(), writes=()):
        o = Op(eng, fn, reads, writes, False, None, 0)
        o.epoch = self.cur_epoch
        self.ops.append(o)
        return o

    def dma(self, eng, fn, reads=(), writes=(), key=None, inc=16, grp=None):
        o = Op(eng, fn, reads, writes, True, key, inc)
        o.grp = grp
        o.epoch = self.cur_epoch
        self.ops.append(o)
        return o

    def _skip(self, d, o):
        if d.dma or o.dma or d.eng != o.eng:
            return False
        if o.eng == "pe":
            return True
        ss = self.self_sync
        if ss is True:
            return False
        if not ss:
            return True
        return o.eng not in ss

    def build(self, stack):
        nc = self.nc
        ops = self.ops
        last_w = {}
        readers = {}
        deps_of = []
        for i, o in enumerate(ops):
            o.idx = i
            deps = set()
            for r in o.reads:
                if r in last_w:
                    deps.add(last_w[r])
            for w in o.writes:
                if w in last_w and not (o.grp is not None and ops[last_w[w]].grp == o.grp):
                    deps.add(last_w[w])
                for rd in readers.get(w, ()):
                    deps.add(rd)
            deps.discard(i)
            for r in o.reads:
                readers.setdefault(r, []).append(i)
            for w in o.writes:
                last_w[w] = i
                readers[w] = []
            deps_of.append(deps)
        for i, o in enumerate(ops):
            for j in deps_of[i]:
                d = ops[j]
                if d.dma or self._skip(d, o):
                    continue
                d.need_inc = True
        eng_sems = {}
        tick = {}
        dma_sems = {}
        dma_cnt = {}
        for o in ops:
            if o.dma:
                if o.dkey not in dma_sems:
                    dma_sems[o.dkey] = stack.enter_context(nc.semaphore("d_" + str(o.dkey)))
                    dma_cnt[o.dkey] = 0
                dma_cnt[o.dkey] += o.dinc
                o.sem = dma_sems[o.dkey]; o.tick = dma_cnt[o.dkey]
            elif o.need_inc:
                k = (o.eng, o.epoch)
                if k not in eng_sems:
                    eng_sems[k] = stack.enter_context(nc.semaphore("e_%s_%d" % k))
                    tick[k] = 0
                tick[k] += 1
                o.sem = eng_sems[k]; o.tick = tick[k]
        known = {e: {} for e in ENGS}
        for i, o in enumerate(ops):
            need = {}
            for j in deps_of[i]:
                d = ops[j]
                if self._skip(d, o) or d.sem is None:
                    continue
                sid = id(d.sem)
                if sid not in need or need[sid][1] < d.tick:
                    need[sid] = (d.sem, d.tick)
            kn = known[o.eng]
            for sid, (s, v) in need.items():
                if kn.get(sid, 0) >= v:
                    continue
                kn[sid] = v
                o.waits.append((s, v))
        self.n_sems = len(eng_sems) + len(dma_sems)
        per = {e: [o for o in ops if o.eng == e] for e in ENGS}
        block = stack.enter_context(nc.Block())

        def emit(engobj, lst):
            for o in lst:
                for (s, v) in o.waits:
                    engobj.wait_ge(s, v)
                if o.fn is None:
                    continue
                ins = o.fn(engobj)
                if o.dma:
                    ins.then_inc(o.sem, o.dinc)
                elif o.need_inc:
                    ins.then_inc(o.sem, 1)

        @block.tensor
        def _(e):
            emit(e, per["pe"])

        @block.scalar
        def _(e):
            emit(e, per["act"])

        @block.vector
        def _(e):
            emit(e, per["dve"])

        @block.gpsimd
        def _(e):
            emit(e, per["pool"])

        @block.sync
        def _(e):
            emit(e, per["sp"])


D = 2048
KC = 16
G = 1024
TT = 512
NT = G // TT
NH = 8
DFF = 5632
NFC = 44
NQ = 4
FQ = NFC // NQ
DEPTH = 4
NSTAGE = 9
EPS = 1e-6
WIN_COLS = 8 * 512 + 1024
V_ADAB = 0
V_MIXG = 96
V_FFNG = 112
V_HNG = 128
V_PSC = 129
V_CW = 137
V_CB = 137 + 132
NV = V_CB + 44
FL_CB = 0
FL_CA = 1
FL_ACT = 2
FL_SEL = 11
NFL = 27
EXW = 1024 + 120 + 88


def build_program(n_stages=NSTAGE, self_sync=("dve",), dbg=False, trunc=99, nl=DEPTH, ncores=8, silu=True, nbanks=5, wsplit=2):
    WSPLIT = wsplit
    nc = bass.Bass("TRN2", target_bir_lowering=False)
    dt = nc.dram_tensor
    xg = dt("xg", [2, G, D], F32, kind="ExternalInput").ap()
    cvec = dt("cvec", [128, 16], F32, kind="ExternalInput").ap()
    ada_w = dt("ada_w", [nl, D, 6 * D], F32, kind="ExternalInput").ap()
    w_in = dt("w_in", [nl, D, WIN_COLS], F32, kind="ExternalInput").ap()
    w_out = dt("w_out", [nl, D, D], F32, kind="ExternalInput").ap()
    w_up = dt("w_up", [nl, D, 2 * DFF], F32, kind="ExternalInput").ap()
    w_down = dt("w_down", [nl, DFF, D], F32, kind="ExternalInput").ap()
    pool_w = dt("pool_w", [nl, 4, 256, 256], F32, kind="ExternalInput").ap()
    vecs = dt("vecs", [nl, 128, NV], F32, kind="ExternalInput").ap()
    hlb = dt("hlb", [128, 4, 8], F32, kind="ExternalInput").ap()
    fng = dt("fng", [128, 16], F32, kind="ExternalInput").ap()
    flags = dt("flags", [128, NFL], F32, kind="ExternalInput").ap()
    invc = dt("invc", [128, 2, 4, 16], F32, kind="ExternalInput").ap()
    maskab = dt("maskab", [128, 2, 512], U8, kind="ExternalInput").ap()
    ident_d = dt("ident", [128, 128], F32, kind="ExternalInput").ap()
    tri_d = dt("tri", [64, 64], F32, kind="ExternalInput").ap()
    rmask_d = dt("rmask", [128, TT], F32, kind="ExternalInput").ap()
    outs = dt("outs", [4, G, D], F32, kind="ExternalOutput").ap()
    exs = [dt("exs%d" % s, [128, EXW], F32, kind="Internal").ap() for s in range(n_stages)]
    exr = [dt("exr%d" % s, [256, EXW], F32, kind="Internal").ap() for s in range(n_stages)]
    dbg_o = dt("dbg_o", [128, 16 * G], F32, kind="ExternalOutput").ap() if dbg else None

    st = ExitStack()
    with st:
        sb = lambda name, shape, dtp: st.enter_context(nc.sbuf_tensor("s_" + name, shape, dtp))
        xT = sb("xT", [128, KC, G], F32)
        hT = sb("hT", [128, KC, G], BF16)
        yT = sb("yT", [128, KC, G], BF16)
        wb = [sb("wb%d" % i, [128, 16 * 512], BF16) for i in range(2)]
        R = [sb("R%d" % i, [128, 528 if i < 6 else 512], F32) for i in range(11)]
        B = [sb("B%d" % i, [128, 512], BF16) for i in range(5)]
        B5 = sb("B5", [128, 512], BF16)
        vtok = sb("vtok", [64, 8, 128], BF16)
        khtok = sb("khtok", [64, 8, 128], BF16)
        pt = sb("pt", [64, 64], BF16)
        Sst = sb("Sst", [128, NH, 128], F32)
        Sbf = sb("Sbf", [128, 128], BF16)
        ptail = sb("ptail", [128, 8, 15], F32)
        ctail = sb("ctail", [128, NFC, 2], F32)
        modT = sb("modT", [128, 4, 96], F32)
        vec = sb("vec", [128, NV], F32)
        der = sb("der", [128, 64], F32)
        lbn = sb("lbn", [128, 4, 8], F32)
        lbt = sb("lbt", [128, 4, 8], F32)
        fngs = sb("fngs", [128, 16], F32)
        flg = sb("flg", [128, NFL], F32)
        invcs = sb("invcs", [128, 2, 4, 16], F32)
        mab = sb("mab", [128, 2, 512], U8)
        identf = sb("identf", [128, 128], F32)
        identb = sb("identb", [128, 128], BF16)
        onesf = sb("onesf", [128, 128], F32)
        tri = sb("tri", [64, 64], F32)
        rmask = sb("rmask", [128, TT], BF16)
        cact = sb("cact", [128, 16], BF16)
        cf = sb("cf", [128, 16], F32)
        pw = sb("pw", [128, 2, 2, 256], BF16)
        ps = lambda name, shape, dtp: st.enter_context(nc.psum_tensor("p_" + name, shape, dtp))
        pj = [ps("pj%d" % i, [128, 512], F32) for i in range(3)]
        pp = ps("pp", [128, 512], F32)
        opb = ps("opb", [128, 512], F32)
        upb = ps("upb", [128, 512], F32)
        tpb = ps("tpb", [128, 1024], BF16)
        nsb = ps("nsb", [128, 512], F32)

        P = Prog(nc, self_sync=self_sync)
        cnt = {"w": 0, "pj": 0}

        def wload(src_ap, ncols_total):
            i = cnt["w"] % 2
            cnt["w"] += 1
            dst = wb[i][:, 0:ncols_total]
            nk = src_ap.shape[0] // 128
            ncol = src_ap.shape[1]
            dv = dst.rearrange("p (k n) -> p k n", n=ncol)
            sv = src_ap.rearrange("(k p) n -> p k n", p=128)
            nsp = WSPLIT if nk >= WSPLIT else 1
            bnd = [nk * t // nsp for t in range(nsp + 1)]
            for t in range(nsp):
                k0, k1 = bnd[t], bnd[t + 1]
                P.dma("pool", lambda e, k0=k0, k1=k1: e.dma_start(out=dv[:, k0:k1, :], in_=sv[:, k0:k1, :]),
                      reads=[], writes=["wb%d" % i], key="wb%d" % i, grp=("w", cnt["w"]))
            return i

        BK = [(pj[0], "pj0"), (pj[1], "pj1"), (pj[2], "pj2"), (opb, "opb"), (upb, "upb")]
        cnt["nb"] = nbanks

        def next_pj():
            i = cnt["pj"] % cnt["nb"]
            cnt["pj"] += 1
            return i

        def mm(out, lhsT, rhs, start, stop, reads, writes):
            P.op("pe", lambda e: e.matmul(out, lhsT=lhsT, rhs=rhs, start=start, stop=stop), reads, writes)

        def proj(wslot, wcol0, tile, dstbank):
            wv = wb[wslot][:].rearrange("p (k n) -> p k n", n=512)
            for k in range(KC):
                mm(BK[dstbank][0][:, :], wv[:, k, wcol0:wcol0 + 128], hT[:, k, tile * TT:(tile + 1) * TT],
                   k == 0, k == KC - 1, ["wb%d" % wslot, ("hT", tile)], [BK[dstbank][1]])

        def act(out, in_, func, reads, writes, bias=None, scale=None):
            kw = {}
            if bias is not None:
                kw["bias"] = bias
            if scale is not None:
                kw["scale"] = scale
            P.op("act", lambda e: e.activation(out=out, in_=in_, func=func, **kw), reads, writes)

        def ts(eng, out, in0, s1, s2, op0, op1, reads, writes):
            P.op(eng, lambda e: e.tensor_scalar(out=out, in0=in0, scalar1=s1, scalar2=s2, op0=op0, op1=op1), reads, writes)

        def stt(eng, out, in0, scalar, in1, op0, op1, reads, writes):
            P.op(eng, lambda e: e.scalar_tensor_tensor(out=out, in0=in0, scalar=scalar, in1=in1, op0=op0, op1=op1), reads, writes)

        def tt(eng, out, in0, in1, op, reads, writes):
            P.op(eng, lambda e: e.tensor_tensor(out=out, in0=in0, in1=in1, op=op), reads, writes)

        def cp(eng, out, in_, reads, writes):
            if eng == "act":
                P.op("act", lambda e: e.activation(out=out, in_=in_, func=AF.Copy), reads, writes)
            else:
                P.op(eng, lambda e: e.tensor_copy(out=out, in_=in_), reads, writes)

        def dcol(j):
            return der[:, j:j + 1]

        lbc0 = sb("lbc0", [128, 8], F32)
        lbc1 = sb("lbc1", [128, 8], F32)
        lbc1n = sb("lbc1n", [128, 8], F32)
        hngh = sb("hngh", [128, 1], F32)

        ld = lambda dst, src, key: P.dma("sp", lambda e: e.dma_start(out=dst, in_=src), reads=[], writes=[key], key="ld_" + key)
        ld(flg[:], flags, "flg")
        ld(invcs[:], invc, "invcs")
        ld(mab[:], maskab, "mab")
        ld(identf[:], ident_d, "identf")
        ld(tri[:], tri_d, "tri")
        P.dma("pool", lambda e: e.dma_start(out=rmask[:], in_=rmask_d), reads=[], writes=["rmask"], key="ld_rmask")
        ld(cf[:], cvec, "cf")
        ld(fngs[:], fng, "fngs")
        ld(lbn[:], hlb, "lbn")
        cp("dve", identb[:], identf[:], ["identf"], ["identb"])
        P.op("dve", lambda e: e.memset(onesf[:], 1.0), [], ["onesf"])
        P.op("dve", lambda e: e.memset(pp[:, :], 0.0), [], ["pp"])
        act(R[0][:, 0:16], cf[:], AF.Tanh, ["cf"], ["R0"], scale=0.5)
        stt("dve", R[0][:, 0:16], R[0][:, 0:16], 1.0, cf[:], ALU.add, ALU.mult, ["R0", "cf"], ["R0"])
        ts("dve", cact[:], R[0][:, 0:16], 0.5, None, ALU.mult, ALU.bypass, ["R0"], ["cact"])
        act(lbn[:], lbn[:], AF.Exp, ["lbn"], ["lbn"])
        tt("dve", R[1][:, 0:8], lbn[:, 0, :], lbn[:, 1, :], ALU.add, ["lbn"], ["R1"])
        tt("dve", R[1][:, 0:8], R[1][:, 0:8], lbn[:, 2, :], ALU.add, ["lbn", "R1"], ["R1"])
        tt("dve", R[1][:, 0:8], R[1][:, 0:8], lbn[:, 3, :], ALU.add, ["lbn", "R1"], ["R1"])
        P.op("dve", lambda e: e.reciprocal(out=R[1][:, 8:16], in_=R[1][:, 0:8]), ["R1"], ["R1"])
        P.op("dve", lambda e: e.memset(lbt[:, 0, :], 0.0), [], ["lbt"])
        tt("dve", lbt[:, 1, :], lbn[:, 1, :], R[1][:, 8:16], ALU.mult, ["lbn", "R1", "lbt"], ["lbt"])
        for l in (2, 3):
            tt("dve", R[1][:, 16:24], lbn[:, l, :], R[1][:, 8:16], ALU.mult, ["lbn", "R1"], ["R1"])
            tt("dve", lbt[:, l, :], lbt[:, l - 1, :], R[1][:, 16:24], ALU.add, ["lbt", "R1"], ["lbt"])
        ts("dve", lbt[:], lbt[:], 1.0, 0.0, ALU.min, ALU.max, ["lbt"], ["lbt"])

        def load_x(grp, pred):
            for tb in range(G // 128):
                for c4 in range(4):
                    stg = R[(tb * 4 + c4) % 2]
                    sk = "R%d" % ((tb * 4 + c4) % 2)
                    src = xg[grp, tb * 128:(tb + 1) * 128, c4 * 512:(c4 + 1) * 512]
                    P.dma("sp", lambda e, stg=stg, src=src: e.dma_start(out=stg[:, 0:512], in_=src), reads=[], writes=[sk], key="ldx_" + sk)
                    bank = next_pj()
                    for j in range(4):
                        P.op("pe", lambda e, j=j, stg=stg, bank=bank: e.transpose(BK[bank][0][:, j * 128:(j + 1) * 128], stg[:, j * 128:(j + 1) * 128], identf[:]),
                             [sk, "identf"], [BK[bank][1]])
                    dst = xT[:, c4 * 4:(c4 + 1) * 4, tb * 128:(tb + 1) * 128]
                    srcp = BK[bank][0][:, :].rearrange("p (j f) -> p j f", f=128)
                    keys = [("xT", c, tb // 4) for c in range(c4 * 4, c4 * 4 + 4)]
                    if pred is None:
                        cp("dve", dst, srcp, [BK[bank][1]], keys)
                    else:
                        m = mab[:, pred, :].rearrange("p (j f) -> p j f", f=128)
                        P.op("dve", lambda e, dst=dst, m=m, srcp=srcp: e.copy_predicated(out=dst, mask=m, data=srcp),
                             [BK[bank][1], "mab"] + keys, keys)

        def norm_to_h(gcol0, shift_ap_fn):
            for tile in range(NT):
                tsl = slice(tile * TT, (tile + 1) * TT)
                for c in range(KC):
                    sq = R[8 + c % 2]
                    sqk = "R%d" % (8 + c % 2)
                    act(sq[:, 0:TT], xT[:, c, tsl], AF.Square, [("xT", c, tile)], [sqk])
                    mm(nsb[:, :], onesf[:], sq[:, 0:TT], c == 0, c == KC - 1, ["onesf", sqk], ["nsb"])
                act(R[10][:, 0:TT], nsb[:, :], AF.Ln, ["nsb"], ["R10"], bias=EPS, scale=1.0 / D)
                act(R[10][:, 0:TT], R[10][:, 0:TT], AF.Exp, ["R10"], ["R10"], scale=-0.5)
                for c in range(KC):
                    tmp = R[6 + c % 2]
                    tk = "R%d" % (6 + c % 2)
                    stt("dve", tmp[:, 0:TT], xT[:, c, tsl], dcol(gcol0 + c), R[10][:, 0:TT], ALU.mult, ALU.mult,
                        [("xT", c, tile), "der", "R10"], [tk])
                    act(hT[:, c, tsl], tmp[:, 0:TT], AF.Identity, [tk, "modT"], [("hT", tile)], bias=shift_ap_fn(c), scale=1.0)

        def final_store(slot):
            for tile in range(NT):
                tsl = slice(tile * TT, (tile + 1) * TT)
                for c in range(KC):
                    sq = R[8 + c % 2]
                    sqk = "R%d" % (8 + c % 2)
                    act(sq[:, 0:TT], xT[:, c, tsl], AF.Square, [("xT", c, tile)], [sqk])
                    mm(nsb[:, :], onesf[:], sq[:, 0:TT], c == 0, c == KC - 1, ["onesf", sqk], ["nsb"])
                act(R[10][:, 0:TT], nsb[:, :], AF.Ln, ["nsb"], ["R10"], bias=EPS, scale=1.0 / D)
                act(R[10][:, 0:TT], R[10][:, 0:TT], AF.Exp, ["R10"], ["R10"], scale=-0.5)
                for c4 in range(4):
                    for cc in range(4):
                        c = c4 * 4 + cc
                        stt("dve", R[cc][:, 0:TT], xT[:, c, tsl], fngs[:, c:c + 1], R[10][:, 0:TT], ALU.mult, ALU.mult,
                            [("xT", c, tile), "fngs", "R10"], ["R%d" % cc])
                    for j in range(4):
                        bank = next_pj()
                        for cc in range(4):
                            P.op("pe", lambda e, j=j, cc=cc, bank=bank: e.transpose(BK[bank][0][:, cc * 128:(cc + 1) * 128], R[cc][:, j * 128:(j + 1) * 128], identf[:]),
                                 ["R%d" % cc, "identf"], [BK[bank][1]])
                        og = R[4 + j % 2]
                        ogk = "R%d" % (4 + j % 2)
                        cp("act", og[:, 0:512], BK[bank][0][:, :], [BK[bank][1]], [ogk])
                        t0 = tile * TT + j * 128
                        dst = outs[slot, t0:t0 + 128, c4 * 512:(c4 + 1) * 512]
                        P.dma("sp", lambda e, og=og, dst=dst: e.dma_start(out=dst, in_=og[:, 0:512]), reads=[ogk], writes=[("outs", slot, tile, c4, j)], key="st_" + ogk)

        def stage(s):
            slot = s % 4
            P.epoch()
            P.dma("sp", lambda e: e.dma_start(out=vec[:], in_=vecs[slot]), reads=[], writes=["vec"], key="ld_vec")
            if s < 4:
                for blk in range(24):
                    wslot = wload(ada_w[slot, :, blk * 512:(blk + 1) * 512], 16 * 512)
                    wv = wb[wslot][:].rearrange("p (k n) -> p k n", n=512)
                    for j4 in range(4):
                        j = blk * 4 + j4
                        for k in range(KC):
                            mm(nsb[:, j:j + 1], wv[:, k, j4 * 128:(j4 + 1) * 128], cact[:, k:k + 1], k == 0, k == KC - 1,
                               ["wb%d" % wslot, "cact"], ["nsb"])
                tt("dve", modT[:, slot, :], nsb[:, 0:96], vec[:, V_ADAB:V_ADAB + 96], ALU.add, ["nsb", "vec"], ["modT"])
            md = lambda i, c: modT[:, slot, i * 16 + c:i * 16 + c + 1]
            stt("dve", der[:, 0:16], modT[:, slot, 16:32], 1.0, vec[:, V_MIXG:V_MIXG + 16], ALU.add, ALU.mult, ["modT", "vec"], ["der"])
            stt("dve", der[:, 16:32], modT[:, slot, 64:80], 1.0, vec[:, V_FFNG:V_FFNG + 16], ALU.add, ALU.mult, ["modT", "vec", "der"], ["der"])
            ts("dve", der[:, 32:48], modT[:, slot, 32:48], flg[:, FL_ACT + s:FL_ACT + s + 1], None, ALU.mult, ALU.bypass, ["modT", "flg", "der"], ["der"])
            ts("dve", der[:, 48:64], modT[:, slot, 80:96], flg[:, FL_ACT + s:FL_ACT + s + 1], (1.0 if silu else 0.5), ALU.mult, ALU.mult, ["modT", "flg", "der"], ["der"])
            ts("dve", lbc0[:], lbt[:, 0, :], flg[:, FL_SEL + slot * 4:FL_SEL + slot * 4 + 1], None, ALU.mult, ALU.bypass, ["lbt", "flg"], ["lbc0"])
            for l in range(1, 4):
                stt("dve", lbc0[:], lbt[:, l, :], flg[:, FL_SEL + slot * 4 + l:FL_SEL + slot * 4 + l + 1], lbc0[:], ALU.mult, ALU.add,
                    ["lbt", "flg", "lbc0"], ["lbc0"])
            ts("dve", lbc1[:], lbc0[:], -0.5, 0.5, ALU.mult, ALU.add, ["lbc0"], ["lbc1"])
            ts("dve", lbc1n[:], lbc1[:], -1.0, None, ALU.mult, ALU.bypass, ["lbc1"], ["lbc1n"])
            ts("dve", lbc0[:], lbc0[:], 0.5, 0.5, ALU.mult, ALU.add, ["lbc0", "lbc1"], ["lbc0"])
            ts("dve", hngh[:], vec[:, V_HNG:V_HNG + 1], 0.5, None, ALU.mult, ALU.bypass, ["vec"], ["hngh"])

            if s == 0:
                P.op("pool", lambda e: e.memset(Sst[:], 0.0), [], ["Sst"])
                P.op("pool", lambda e: e.memset(ptail[:], 0.0), [], ["ptail"])
                P.op("pool", lambda e: e.memset(ctail[:], 0.0), [], ["ctail"])
            else:
                Sfl = Sst[:].rearrange("p h v -> p (h v)")
                ptf = ptail[:].rearrange("p c t -> p (c t)")
                ctf = ctail[:].rearrange("p c t -> p (c t)")
                src = exr[s - 1]
                P.dma("sp", lambda e: e.dma_start(out=Sfl, in_=src[0:128, 0:1024]), reads=["exr%d" % (s - 1)], writes=["Sst"], key="ld_S")
                P.dma("sp", lambda e: e.dma_start(out=ptf, in_=src[0:128, 1024:1144]), reads=["exr%d" % (s - 1)], writes=["ptail"], key="ld_pt")
                P.dma("sp", lambda e: e.dma_start(out=ctf, in_=src[0:128, 1144:1232]), reads=["exr%d" % (s - 1)], writes=["ctail"], key="ld_ct")
                cb = flg[:, FL_CB:FL_CB + 1]
                ts("dve", Sfl, Sfl, cb, None, ALU.mult, ALU.bypass, ["Sst", "flg"], ["Sst"])
                ts("dve", ptf, ptf, cb, None, ALU.mult, ALU.bypass, ["ptail", "flg"], ["ptail"])
                ts("dve", ctf, ctf, cb, None, ALU.mult, ALU.bypass, ["ctail", "flg"], ["ctail"])
                if s >= 4:
                    src2 = exr[s - 3]
                    ca = flg[:, FL_CA:FL_CA + 1]
                    P.dma("sp", lambda e: e.dma_start(out=R[0][:, 0:512], in_=src2[128:256, 0:512]), reads=["exr%d" % (s - 3)], writes=["R0"], key="ld_R0")
                    P.dma("sp", lambda e: e.dma_start(out=R[1][:, 0:512], in_=src2[128:256, 512:1024]), reads=["exr%d" % (s - 3)], writes=["R1"], key="ld_R1")
                    P.dma("sp", lambda e: e.dma_start(out=R[2][:, 0:208], in_=src2[128:256, 1024:1232]), reads=["exr%d" % (s - 3)], writes=["R2"], key="ld_R2")
                    stt("dve", Sfl[:, 0:512], R[0][:, 0:512], ca, Sfl[:, 0:512], ALU.mult, ALU.add, ["R0", "flg", "Sst"], ["Sst"])
                    stt("dve", Sfl[:, 512:1024], R[1][:, 0:512], ca, Sfl[:, 512:1024], ALU.mult, ALU.add, ["R1", "flg", "Sst"], ["Sst"])
                    stt("dve", ptf, R[2][:, 0:120], ca, ptf, ALU.mult, ALU.add, ["R2", "flg", "ptail"], ["ptail"])
                    stt("dve", ctf, R[2][:, 120:208], ca, ctf, ALU.mult, ALU.add, ["R2", "flg", "ctail"], ["ctail"])

            norm_to_h(0, lambda c: md(0, c))

            if trunc <= 1:
                return
            QSC = 0.5 * (128.0 ** -0.5)
            cnt["nb"] = 3
            f32v = lambda ch: yT[:, ch, :].bitcast(F32)
            yk = lambda ch: [("yT", ch, 0), ("yT", ch, 1)]
            SETS = [
                dict(ktd=(B[0][:, 0:TT], ["B0"]), qtd=(B[1][:, 0:TT], ["B1"]), qh=(B[2][:, 0:TT], ["B2"]), oo=(B[3][:, 0:TT], ["B3"]),
                     vtok=(vtok[:], ["vtok"]), khtok=(khtok[:], ["khtok"]), e3=(R[10][:, 0:TT], ["R10"]), gs=(R[5][:, 0:TT], ["R5"])),
                dict(ktd=(yT[:, 8, 0:512], [("yT", 8, 0)]), qtd=(yT[:, 8, 512:1024], [("yT", 8, 1)]),
                     qh=(yT[:, 9, 0:512], [("yT", 9, 0)]), oo=(yT[:, 9, 512:1024], [("yT", 9, 1)]),
                     vtok=(yT[0:64, 10, :].rearrange("p (c v) -> p c v", v=128), yk(10)),
                     khtok=(yT[0:64, 11, :].rearrange("p (c v) -> p c v", v=128), yk(11)),
                     e3=(f32v(12), yk(12)), gs=(f32v(13), yk(13))),
            ]
            NT1, NT1k = f32v(14), yk(14)
            NT2, NT2k = f32v(15), yk(15)
            units = [(h, tile) for h in range(NH) for tile in range(NT)]
            wslots = {}
            abank = {}

            def A_part(n, part):
                h, tile = units[n]
                S = SETS[n % 2]
                tsl = slice(tile * TT, (tile + 1) * TT)
                if part == 0 and tile == 0:
                    if h == 0:
                        wslots[0] = wload(w_in[slot, :, 0:512], 16 * 512)
                    if h + 1 < NH:
                        wslots[h + 1] = wload(w_in[slot, :, (h + 1) * 512:(h + 2) * 512], 16 * 512)
                wslot = wslots[h]
                wv = wb[wslot][:].rearrange("p (k n) -> p k n", n=512)
                j = part // 2
                col = (128, 0, 384, 256)[j]
                if part % 2 == 0:
                    abank[(n, j)] = next_pj()
                bk = abank[(n, j)]
                for k in range((part % 2) * 8, (part % 2) * 8 + 8):
                    mm(BK[bk][0][:, :], wv[:, k, col:col + 128], hT[:, k, tsl], k == 0, k == KC - 1,
                       ["wb%d" % wslot, ("hT", tile)], [BK[bk][1]])
                if part % 2 == 0:
                    return
                pk = BK[bk][1]
                pb_ = BK[bk][0]
                c0 = lbc0[:, h:h + 1]; c1 = lbc1[:, h:h + 1]; c1n = lbc1n[:, h:h + 1]
                if j == 0:
                    act(R[0][:, 0:TT], pb_[:, :], AF.Tanh, [pk], ["R0"], scale=0.5)
                    ts("dve", R[1][:, 0:TT], R[0][:, 0:TT], c1, c0, ALU.mult, ALU.add, ["R0", "lbc0", "lbc1"], ["R1"])
                    ts("dve", R[2][:, 0:TT], R[0][:, 0:TT], c1n, c1, ALU.mult, ALU.add, ["R0", "lbc1", "lbc1n"], ["R2"])
                    act(R[0][:, 0:TT], R[1][:, 0:TT], AF.Ln, ["R1"], ["R0"])
                    P.op("dve", lambda e: e.tensor_tensor_scan(out=R[3][:, 0:TT], data0=rmask[:], data1=R[0][:, 0:TT], initial=0.0, op0=ALU.mult, op1=ALU.add),
                         ["rmask", "R0"], ["R3"])
                elif j == 1:
                    act(R[4][:, 0:TT], pb_[:, :], AF.Tanh, [pk], ["R4"], scale=0.5)
                    stt("dve", R[4][:, 0:TT], R[4][:, 0:TT], 1.0, pb_[:, :], ALU.add, ALU.mult, ["R4", pk], ["R4"])
                    b32 = R[3][:, 0:TT].rearrange("p (n j) -> p n j", j=32)
                    b64 = R[3][:, 0:TT].rearrange("p (n j) -> p n j", j=64)
                    v32 = lambda r: r[:, 0:TT].rearrange("p (n j) -> p n j", j=32)
                    v64 = lambda r: r[:, 0:TT].rearrange("p (n j) -> p n j", j=64)
                    vxy = lambda r: r[:, 0:TT].rearrange("p (n b j) -> p n b j", b=2, j=32)
                    axy = lambda a: a.rearrange("p (n b j) -> p n b j", b=2, j=32)
                    ktd, ktdk = S["ktd"]; qtd, qtdk = S["qtd"]; qh, qhk = S["qh"]; oo, ook = S["oo"]; e3, e3k = S["e3"]
                    tt("dve", v32(R[6]), b32, b32[:, :, 15:16].to_broadcast([128, 16, 32]), ALU.subtract, ["R3"], ["R6"])
                    act(R[8][:, 0:TT], R[6][:, 0:TT], AF.Exp, ["R6"], ["R8"])
                    stt("dve", qtd, R[4][:, 0:TT], QSC, R[8][:, 0:TT], ALU.mult, ALU.mult, ["R4", "R8"], qtdk)
                    act(R[9][:, 0:TT], R[6][:, 0:TT], AF.Exp, ["R6"], ["R9"], scale=-1.0)
                    tt("dve", ktd, R[2][:, 0:TT], R[9][:, 0:TT], ALU.mult, ["R2", "R9"], ktdk)
                    tt("dve", v64(R[7]), b64, b64[:, :, 31:32].to_broadcast([128, 8, 64]), ALU.subtract, ["R3"], ["R7"])
                    act(vxy(R[8])[:, :, 1, :], vxy(R[7])[:, :, 1, :], AF.Exp, ["R7"], ["R8"])
                    act(vxy(R[8])[:, :, 0, :], vxy(R[7])[:, :, 0, :], AF.Exp, ["R7"], ["R8"], scale=-1.0)
                    stt("dve", oo[:, 0:256].rearrange("p (n j) -> p n j", j=32), vxy(R[4])[:, :, 1, :], QSC, vxy(R[8])[:, :, 1, :], ALU.mult, ALU.mult,
                        ["R4", "R8"], ook)
                    tt("dve", oo[:, 256:512].rearrange("p (n j) -> p n j", j=32), vxy(R[2])[:, :, 0, :], vxy(R[8])[:, :, 0, :], ALU.mult,
                       ["R2", "R8"] + ook, ook)
                    act(e3, R[3][:, 0:TT], AF.Exp, ["R3"], e3k)
                    stt("dve", qh, R[4][:, 0:TT], QSC, e3, ALU.mult, ALU.mult, ["R4"] + e3k, qhk)
                    tt("dve", v64(R[6]), b64, b64[:, :, 63:64].to_broadcast([128, 8, 64]), ALU.subtract, ["R3", "R6"], ["R6"])
                    act(R[9][:, 0:TT], R[6][:, 0:TT], AF.Exp, ["R6"], ["R9"], scale=-1.0)
                    tt("dve", B[4][:, 0:TT], R[2][:, 0:TT], R[9][:, 0:TT], ALU.mult, ["R2", "R9"], ["B4"])
                elif j == 2:
                    gs, gsk = S["gs"]
                    act(gs, pb_[:, :], AF.Tanh, [pk], gsk, scale=0.5)
                    stt("dve", gs, gs, 1.0, pb_[:, :], ALU.add, ALU.mult, gsk + [pk], gsk)
                else:
                    vt, vtk = S["vtok"]; kt_, ktk = S["khtok"]
                    cp("act", B5[:, 0:TT], pb_[:, :], [pk], ["B5"])
                    for c in range(8):
                        P.op("pe", lambda e, c=c: e.transpose(tpb[0:64, c * 128:(c + 1) * 128], B5[:, c * 64:(c + 1) * 64], identb[:]),
                             ["B5", "identb"], ["tpb"])
                    cp("act", vt.rearrange("p c v -> p (c v)"), tpb[0:64, :], ["tpb"], vtk)
                    for c in range(8):
                        P.op("pe", lambda e, c=c: e.transpose(tpb[0:64, c * 128:(c + 1) * 128], B[4][:, c * 64:(c + 1) * 64], identb[:]),
                             ["B4", "identb"], ["tpb"])
                    cp("act", kt_.rearrange("p c v -> p (c v)"), tpb[0:64, :], ["tpb"], ktk)

            def B_chunk(n, c):
                h, tile = units[n]
                S = SETS[n % 2]
                ktd, ktdk = S["ktd"]; qtd, qtdk = S["qtd"]; qh, qhk = S["qh"]; oo, ook = S["oo"]; e3, e3k = S["e3"]
                vt, vtk = S["vtok"]; kt_, ktk = S["khtok"]
                kt32 = ktd.rearrange("p (n b j) -> p n b j", b=2, j=32)
                qt32 = qtd.rearrange("p (n b j) -> p n b j", b=2, j=32)
                qo32 = oo[:, 0:256].rearrange("p (n j) -> p n j", j=32)
                ko32 = oo[:, 256:512].rearrange("p (n j) -> p n j", j=32)
                if c == 0 and tile == 0:
                    cp("act", Sbf[:], Sst[:, h, :], ["Sst"], ["Sbf"])
                mm(pp[0:32, 0:32], kt32[:, c, 0, :], qt32[:, c, 0, :], True, True, ktdk + qtdk, ["pp"])
                mm(pp[0:32, 32:64], ko32[:, c, :], qo32[:, c, :], True, True, ook, ["pp"])
                mm(pp[32:64, 32:64], kt32[:, c, 1, :], qt32[:, c, 1, :], True, True, ktdk + qtdk, ["pp"])
                tt("dve", pt[:], pp[0:64, 0:64], tri[:], ALU.mult, ["pp", "tri"], ["pt"])
                mm(opb[:, c * 64:(c + 1) * 64], vt[:, c, :], pt[:], True, False, vtk + ["pt"], ["opb"])
                mm(opb[:, c * 64:(c + 1) * 64], Sbf[:], qh[:, c * 64:(c + 1) * 64], False, True, ["Sbf"] + qhk, ["opb"])
                mm(upb[:, 0:128], kt_[:, c, :], vt[:, c, :], True, True, ktk + vtk, ["upb"])
                stt("dve", Sbf[:], Sst[:, h, :], e3[:, c * 64 + 63:c * 64 + 64], upb[:, 0:128], ALU.mult, ALU.add,
                    ["Sst", "upb"] + e3k, ["Sbf"])
                stt("dve", Sst[:, h, :], Sst[:, h, :], e3[:, c * 64 + 63:c * 64 + 64], upb[:, 0:128], ALU.mult, ALU.add,
                    ["Sst", "upb"] + e3k, ["Sst"])

            def B_finish(n):
                h, tile = units[n]
                S = SETS[n % 2]
                gs, gsk = S["gs"]
                tsl = slice(tile * TT, (tile + 1) * TT)
                act(NT1, opb[:, :], AF.Square, ["opb"], NT1k)
                mm(nsb[:, :], onesf[:], NT1, True, True, ["onesf"] + NT1k, ["nsb"])
                act(NT2, nsb[:, :], AF.Ln, ["nsb"], NT2k, bias=EPS, scale=1.0 / 128.0)
                act(NT2, NT2, AF.Exp, NT2k, NT2k, scale=-0.5)
                stt("dve", NT1, opb[:, :], hngh[:, 0:1], NT2, ALU.mult, ALU.mult, ["opb", "hngh"] + NT2k + NT1k, NT1k)
                tt("dve", yT[:, h, tsl], NT1, gs, ALU.mult, NT1k + gsk, [("yT", h, tile)])

            for part in range(8):
                A_part(0, part)
            for n in range(len(units)):
                for c in range(8):
                    B_chunk(n, c)
                    if n + 1 < len(units):
                        A_part(n + 1, c)
                B_finish(n)

            if trunc <= 2:
                return
            cnt["nb"] = nbanks
            for pb in range(2):
                wslot = wload(w_in[slot, :, 4096 + pb * 512:4096 + (pb + 1) * 512], 16 * 512)
                P.dma("pool", lambda e, pb=pb: e.dma_start(out=pw[:], in_=pool_w[slot, pb * 2:(pb + 1) * 2].rearrange("g (dc p) e -> p g dc e", p=128)),
                      reads=[], writes=["pw"], key="ld_pw")
                for gl in range(2):
                    g = pb * 2 + gl
                    nlev = g + 1
                    invw = 1.0 / (2 ** nlev)
                    for tile in range(NT):
                        tsl = slice(tile * TT, (tile + 1) * TT)
                        for dc in range(2):
                            ch = g * 2 + dc
                            bu = next_pj(); proj(wslot, (gl * 2 + dc) * 128, tile, bu)
                            U = R[dc]; Uk = "R%d" % dc
                            cp("pool", U[:, 0:15], ptail[:, ch, :], ["ptail"], [Uk])
                            cp("act", U[:, 15:15 + TT], BK[bu][0][:, :], [BK[bu][1], Uk], [Uk])
                            cp("pool", ptail[:, ch, :], U[:, TT:TT + 15], [Uk, "ptail"], ["ptail"])
                            prev = U; prevk = Uk
                            for lev in range(nlev):
                                sh = 2 ** lev
                                nx = R[2 + dc * 2 + lev % 2]; nxk = "R%d" % (2 + dc * 2 + lev % 2)
                                lo = 2 ** (lev + 1) - 1
                                tt("dve", nx[:, lo:15 + TT], prev[:, lo:15 + TT], prev[:, lo - sh:15 + TT - sh], ALU.add,
                                   [prevk, nxk], [nxk])
                                prev = nx; prevk = nxk
                            if tile == 0:
                                iv = invcs[:, 0 if s < 4 else 1, g, :]
                                tt("dve", R[6 + dc][:, 0:16], prev[:, 15:31], iv, ALU.mult, [prevk, "invcs", "R%d" % (6 + dc)], ["R%d" % (6 + dc)])
                                tt("dve", B[dc][:, 0:16], R[6 + dc][:, 0:16], U[:, 15:31], ALU.subtract, ["R%d" % (6 + dc), Uk], ["B%d" % dc])
                                stt("dve", B[dc][:, 16:TT], prev[:, 31:15 + TT], invw, U[:, 31:15 + TT], ALU.mult, ALU.subtract, [prevk, Uk, "B%d" % dc], ["B%d" % dc])
                            else:
                                stt("dve", B[dc][:, 0:TT], prev[:, 15:15 + TT], invw, U[:, 15:15 + TT], ALU.mult, ALU.subtract, [prevk, Uk], ["B%d" % dc])
                        for ec in range(2):
                            bo = next_pj()
                            for dc in range(2):
                                mm(BK[bo][0][:, :], pw[:, gl, dc, ec * 128:(ec + 1) * 128], B[dc][:, 0:TT], dc == 0, dc == 1, ["pw", "B%d" % dc], [BK[bo][1]])
                            ych = 8 + g * 2 + ec
                            act(yT[:, ych, tsl], BK[bo][0][:, :], AF.Identity, [BK[bo][1], "vec"], [("yT", ych, tile)],
                                scale=vec[:, V_PSC + g * 2 + ec:V_PSC + g * 2 + ec + 1])

            if trunc <= 3:
                return
            for ob in range(4):
                wslot = wload(w_out[slot, :, ob * 512:(ob + 1) * 512], 16 * 512)
                wv = wb[wslot][:].rearrange("p (k n) -> p k n", n=512)
                for oc in range(4):
                    c = ob * 4 + oc
                    for tile in range(NT):
                        tsl = slice(tile * TT, (tile + 1) * TT)
                        bo = next_pj()
                        for k in range(KC):
                            mm(BK[bo][0][:, :], wv[:, k, oc * 128:(oc + 1) * 128], yT[:, k, tsl], k == 0, k == KC - 1,
                               ["wb%d" % wslot, ("yT", k, tile)], [BK[bo][1]])
                        stt("dve", xT[:, c, tsl], BK[bo][0][:, :], dcol(32 + c), xT[:, c, tsl], ALU.mult, ALU.add,
                            [BK[bo][1], "der", ("xT", c, tile)], [("xT", c, tile)])

            if trunc <= 4:
                return
            norm_to_h(16, lambda c: md(3, c))

            if trunc <= 5:
                return
            for q in range(NQ):
                for jb in range(FQ // 2 + 1):
                    fcs = [q * FQ + jb * 2 + i for i in range(2) if jb * 2 + i < FQ]
                    col0 = fcs[0] * 256
                    ncol = len(fcs) * 256
                    wslot = wload(w_up[slot, :, col0:col0 + ncol], 16 * ncol)
                    wv = wb[wslot][:, 0:16 * ncol].rearrange("p (k n) -> p k n", n=ncol)
                    for i, fc in enumerate(fcs):
                        fl = fc - q * FQ
                        for tile in range(NT):
                            tsl = slice(tile * TT, (tile + 1) * TT)
                            ba = next_pj()
                            for k in range(KC):
                                mm(BK[ba][0][:, :], wv[:, k, i * 256:i * 256 + 128], hT[:, k, tsl], k == 0, k == KC - 1,
                                   ["wb%d" % wslot, ("hT", tile)], [BK[ba][1]])
                            bv = next_pj()
                            for k in range(KC):
                                mm(BK[bv][0][:, :], wv[:, k, i * 256 + 128:i * 256 + 256], hT[:, k, tsl], k == 0, k == KC - 1,
                                   ["wb%d" % wslot, ("hT", tile)], [BK[bv][1]])
                            par = (fl * NT + tile) % 2
                            A = R[par]; Ak = "R%d" % par
                            C = R[2 + par]; Ck = "R%d" % (2 + par)
                            Tn = R[4 + par]; Tk = "R%d" % (4 + par)
                            cp("pool", A[:, 0:2], ctail[:, fc, :], ["ctail"], [Ak])
                            cp("act", A[:, 2:2 + TT], BK[ba][0][:, :], [BK[ba][1], Ak], [Ak])
                            cp("pool", ctail[:, fc, :], A[:, TT:TT + 2], [Ak, "ctail"], ["ctail"])
                            cw = lambda tap: vec[:, V_CW + tap * NFC + fc:V_CW + tap * NFC + fc + 1]
                            act(C[:, 0:TT], BK[ba][0][:, :], AF.Identity, [BK[ba][1], "vec"], [Ck], bias=vec[:, V_CB + fc:V_CB + fc + 1], scale=cw(2))
                            stt("dve", C[:, 0:TT], A[:, 1:1 + TT], cw(1), C[:, 0:TT], ALU.mult, ALU.add, [Ak, "vec", Ck], [Ck])
                            stt("dve", C[:, 0:TT], A[:, 0:TT], cw(0), C[:, 0:TT], ALU.mult, ALU.add, [Ak, "vec", Ck], [Ck])
                            if silu:
                                act(Tn[:, 0:TT], C[:, 0:TT], AF.Silu, [Ck], [Tk])
                            else:
                                act(Tn[:, 0:TT], C[:, 0:TT], AF.Tanh, [Ck], [Tk], scale=0.5)
                                stt("dve", Tn[:, 0:TT], Tn[:, 0:TT], 1.0, C[:, 0:TT], ALU.add, ALU.mult, [Tk, Ck], [Tk])
                            tt("dve", yT[:, fl, tsl], Tn[:, 0:TT], BK[bv][0][:, :], ALU.mult, [Tk, BK[bv][1]], [("yT", fl, tile)])
                for ob in range(4):
                    wslot = wload(w_down[slot, q * FQ * 128:(q + 1) * FQ * 128, ob * 512:(ob + 1) * 512], FQ * 512)
                    wv = wb[wslot][:, 0:FQ * 512].rearrange("p (k n) -> p k n", n=512)
                    for oc in range(4):
                        c = ob * 4 + oc
                        for tile in range(NT):
                            tsl = slice(tile * TT, (tile + 1) * TT)
                            bo = next_pj()
                            for k in range(FQ):
                                mm(BK[bo][0][:, :], wv[:, k, oc * 128:(oc + 1) * 128], yT[:, k, tsl], k == 0, k == FQ - 1,
                                   ["wb%d" % wslot, ("yT", k, tile)], [BK[bo][1]])
                            stt("dve", xT[:, c, tsl], BK[bo][0][:, :], dcol(48 + c), xT[:, c, tsl], ALU.mult, ALU.add,
                                [BK[bo][1], "der", ("xT", c, tile)], [("xT", c, tile)])

            Sfl = Sst[:].rearrange("p h v -> p (h v)")
            P.dma("sp", lambda e: e.dma_start(out=exs[s][:, 0:1024], in_=Sfl), reads=["Sst"], writes=["exs%d" % s], key="exs_a")
            P.dma("sp", lambda e: e.dma_start(out=exs[s][:, 1024:1144], in_=ptail[:].rearrange("p c t -> p (c t)")), reads=["ptail"], writes=["exs%d" % s], key="exs_b")
            P.dma("sp", lambda e: e.dma_start(out=exs[s][:, 1144:1232], in_=ctail[:].rearrange("p c t -> p (c t)")), reads=["ctail"], writes=["exs%d" % s], key="exs_c")
            P.dma("pool", lambda e: e.collective_compute("AllGather", ALU.bypass, replica_groups=[[2 * i, 2 * i + 1] for i in range(ncores // 2)],
                                                          ins=[exs[s]], outs=[exr[s]]),
                  reads=["exs%d" % s], writes=["exr%d" % s], key="cc", inc=1)

        load_x(0, None)
        for s in range(n_stages):
            if s == 4:
                load_x(1, 0)
            if s == 5:
                load_x(1, 1)
            stage(s)
            if dbg and s == n_stages - 1:
                P.dma("sp", lambda e: e.dma_start(out=dbg_o, in_=xT[:].rearrange("p c t -> p (c t)")),
                      reads=[("xT", c, t) for c in range(KC) for t in range(NT)], writes=["dbg_o"], key="dbg")
            if s in (3, 4, 7, 8):
                final_store({3: 0, 4: 1, 7: 2, 8: 3}[s])
        P.op("sp", None, reads=[("outs", sl, tl, c4, j) for sl in range(4) for tl in range(NT) for c4 in range(4) for j in range(4)] + ["dbg_o"] + ["exr%d" % s for s in range(n_stages)])
        P.build(st)
    return nc


def _colmajor(v):
    return np.ascontiguousarray(np.asarray(v).reshape(-1, 128).T)


def prep_inputs(x, c, ada_w, ada_b, mix_norm_g, w_in, hgrn_lower_bounds, hgrn_norm_g, pool_w,
                pool_scale, w_out, ffn_norm_g, w_up, conv_w, conv_b, w_down, final_norm_g):
    f32 = np.float32
    x = np.asarray(x, f32); c = np.asarray(c, f32)
    perm_in = []
    for h in range(NH):
        for blk in range(4):
            perm_in.extend(range(blk * 1024 + h * 128, blk * 1024 + (h + 1) * 128))
    perm_in.extend(range(4096, 5120))
    perm_in = np.array(perm_in)
    perm_up = []
    for fc in range(NFC):
        perm_up.extend(range(fc * 128, (fc + 1) * 128))
        perm_up.extend(range(DFF + fc * 128, DFF + (fc + 1) * 128))
    perm_up = np.array(perm_up)
    w_in_p = np.ascontiguousarray(np.asarray(w_in, f32)[:, :, perm_in])
    w_up_p = np.ascontiguousarray(np.asarray(w_up, f32)[:, :, perm_up])
    vecs = np.zeros((DEPTH, 128, NV), f32)
    for l in range(DEPTH):
        vecs[l, :, V_ADAB:V_ADAB + 96] = _colmajor(ada_b[l])
        vecs[l, :, V_MIXG:V_MIXG + 16] = _colmajor(mix_norm_g[l])
        vecs[l, :, V_FFNG:V_FFNG + 16] = _colmajor(ffn_norm_g[l])
        vecs[l, :, V_HNG] = np.asarray(hgrn_norm_g[l])
        vecs[l, :, V_PSC:V_PSC + 8] = _colmajor(pool_scale[l])
        for tap in range(3):
            vecs[l, :, V_CW + tap * NFC:V_CW + (tap + 1) * NFC] = _colmajor(conv_w[l, tap])
        vecs[l, :, V_CB:V_CB + NFC] = _colmajor(conv_b[l])
    hlb = np.ascontiguousarray(np.asarray(hgrn_lower_bounds, f32).reshape(4, 8, 128).transpose(2, 0, 1))
    fng = _colmajor(final_norm_g).astype(f32)
    ident = np.eye(128, dtype=f32)
    tri = np.triu(np.ones((64, 64), f32))
    rmask = np.ones((128, TT), f32); rmask[:, ::64] = 0.0
    roll = lambda a: np.ascontiguousarray(np.roll(np.asarray(a, f32), 1, axis=0))
    shared = {
        0: dict(ada_w=np.asarray(ada_w, f32), w_in=w_in_p, w_out=np.asarray(w_out, f32), w_up=w_up_p,
                w_down=np.asarray(w_down, f32), pool_w=np.asarray(pool_w, f32), vecs=vecs),
    }
    shared[1] = {k: roll(v) for k, v in shared[0].items()}
    in_maps = []
    for core in range(8):
        b = core // 2; role = core % 2
        xg = np.ascontiguousarray(np.stack([x[b, (role + 0) * G:(role + 1) * G], x[b, (role + 2) * G:(role + 3) * G]]))
        flags = np.zeros((128, NFL), f32)
        flags[:, FL_CB] = 1.0 if role == 1 else 0.0
        flags[:, FL_CA] = 1.0 if role == 0 else 0.0
        for s in range(NSTAGE):
            active = (s <= 7) if role == 0 else (s >= 1)
            flags[:, FL_ACT + s] = 1.0 if active else 0.0
        for slot in range(4):
            layer = (slot - role) % 4
            flags[:, FL_SEL + slot * 4 + layer] = 1.0
        invc = np.zeros((128, 2, 4, 16), f32)
        for g in range(4):
            w = 2 ** (g + 1)
            invc[:, :, g, :] = 1.0 / w
            if role == 0:
                invc[:, 0, g, :] = 1.0 / np.minimum(np.arange(1, 17), w)
        maskab = np.zeros((128, 2, 512), np.uint8)
        maskab[:, role, :] = 1
        m = dict(shared[role])
        m.update(xg=xg, cvec=_colmajor(c[b]).astype(f32), hlb=hlb, fng=fng, flags=flags, invc=invc,
                 maskab=maskab, ident=ident, tri=tri, rmask=rmask)
        in_maps.append(m)
    return in_maps


def assemble(results):
    out = np.zeros((4, 4 * G, D), np.float32)
    for core in range(8):
        b = core // 2; role = core % 2
        o = results[core]["outs"]
        if role == 0:
            out[b, 0:G] = o[0]; out[b, 2 * G:3 * G] = o[2]
        else:
            out[b, G:2 * G] = o[1]; out[b, 3 * G:4 * G] = o[3]
    return out


_NC_CACHE = {}


def kernel(**inputs):
    in_maps = prep_inputs(**inputs)
    if "nc" not in _NC_CACHE:
        _NC_CACHE["nc"] = build_program()
    res = run_bass_kernel_spmd(_NC_CACHE["nc"], in_maps, core_ids=list(range(8)))
    return assemble(res.results)
```

```python
import numpy as np
from contextlib import ExitStack
import concourse.bass as bass
import concourse.mybir as mybir
from concourse.bass_utils import run_bass_kernel_spmd

F32 = mybir.dt.float32
BF16 = mybir.dt.bfloat16
I32 = mybir.dt.int32
U8 = mybir.dt.uint8
AF = mybir.ActivationFunctionType
ALU = mybir.AluOpType

ENGS = ("pe", "act", "dve", "pool", "sp")


class Op:
    __slots__ = ("eng", "fn", "reads", "writes", "dma", "dkey", "dinc", "idx",
                 "need_inc", "sem", "tick", "waits", "epoch", "grp")

    def __init__(self, eng, fn, reads, writes, dma, dkey, dinc):
        self.eng = eng; self.fn = fn; self.reads = tuple(reads); self.writes = tuple(writes)
        self.dma = dma; self.dkey = dkey; self.dinc = dinc
        self.need_inc = False; self.sem = None; self.tick = 0; self.waits = []; self.epoch = 0; self.grp = None


class Prog:
    def __init__(self, nc, self_sync=True):
        self.nc = nc
        self.ops = []
        self.self_sync = self_sync
        self.cur_epoch = 0

    def epoch(self):
        self.cur_epoch += 1

    def op(self, eng, fn, reads=(), writes=()):
        o = Op(eng, fn, reads, writes, False, None, 0)
        o.epoch = self.cur_epoch
        self.ops.append(o)
        return o

    def dma(self, eng, fn, reads=(), writes=(), key=None, inc=16, grp=None):
        o = Op(eng, fn, reads, writes, True, key, inc)
        o.grp = grp
        o.epoch = self.cur_epoch
        self.ops.append(o)
        return o

    def _skip(self, d, o):
        if d.dma or o.dma or d.eng != o.eng:
            return False
        if o.eng == "pe":
            return True
        ss = self.self_sync
        if ss is True:
            return False
        if not ss:
            return True
        return o.eng not in ss

    def build(self, stack):
        nc = self.nc
        ops = self.ops
        last_w = {}
        readers = {}
        deps_of = []
        for i, o in enumerate(ops):
            o.idx = i
            deps = set()
            for r in o.reads:
                if r in last_w:
                    deps.add(last_w[r])
            for w in o.writes:
                if w in last_w and not (o.grp is not None and ops[last_w[w]].grp == o.grp):
                    deps.add(last_w[w])
                for rd in readers.get(w, ()):
                    deps.add(rd)
            deps.discard(i)
            for r in o.reads:
                readers.setdefault(r, []).append(i)
            for w in o.writes:
                last_w[w] = i
                readers[w] = []
            deps_of.append(deps)
        for i, o in enumerate(ops):
            for j in deps_of[i]:
                d = ops[j]
                if d.dma or self._skip(d, o):
                    continue
                d.need_inc = True
        eng_sems = {}
        tick = {}
        dma_sems = {}
        dma_cnt = {}
        for o in ops:
            if o.dma:
                if o.dkey not in dma_sems:
                    dma_sems[o.dkey] = stack.enter_context(nc.semaphore("d_" + str(o.dkey)))
                    dma_cnt[o.dkey] = 0
                dma_cnt[o.dkey] += o.dinc
                o.sem = dma_sems[o.dkey]; o.tick = dma_cnt[o.dkey]
            elif o.need_inc:
                k = (o.eng, o.epoch)
                if k not in eng_sems:
                    eng_sems[k] = stack.enter_context(nc.semaphore("e_%s_%d" % k))
                    tick[k] = 0
                tick[k] += 1
                o.sem = eng_sems[k]; o.tick = tick[k]
        known = {e: {} for e in ENGS}
        for i, o in enumerate(ops):
            need = {}
            for j in deps_of[i]:
                d = ops[j]
                if self._skip(d, o) or d.sem is None:
                    continue
                sid = id(d.sem)
                if sid not in need or need[sid][1] < d.tick:
                    need[sid] = (d.sem, d.tick)
            kn = known[o.eng]
            for sid, (s, v) in need.items():
                if kn.get(sid, 0) >= v:
                    continue
                kn[sid] = v
                o.waits.append((s, v))
        self.n_sems = len(eng_sems) + len(dma_sems)
        per = {e: [o for o in ops if o.eng == e] for e in ENGS}
        block = stack.enter_context(nc.Block())

        def emit(engobj, lst):
            for o in lst:
                for (s, v) in o.waits:
                    engobj.wait_ge(s, v)
                if o.fn is None:
                    continue
                ins = o.fn(engobj)
                if o.dma:
                    ins.then_inc(o.sem, o.dinc)
                elif o.need_inc:
                    ins.then_inc(o.sem, 1)

        @block.tensor
        def _(e):
            emit(e, per["pe"])

        @block.scalar
        def _(e):
            emit(e, per["act"])

        @block.vector
        def _(e):
            emit(e, per["dve"])

        @block.gpsimd
        def _(e):
            emit(e, per["pool"])

        @block.sync
        def _(e):
            emit(e, per["sp"])


D = 2048
KC = 16
G = 1024
TT = 512
NT = G // TT
NH = 8
DFF = 5632
NFC = 44
NQ = 4
FQ = NFC // NQ
DEPTH = 4
NSTAGE = 9
EPS = 1e-6
WIN_COLS = 8 * 512 + 1024
V_ADAB = 0
V_MIXG = 96
V_FFNG = 112
V_HNG = 128
V_PSC = 129
V_CW = 137
V_CB = 137 + 132
NV = V_CB + 44
FL_CB = 0
FL_CA = 1
FL_ACT = 2
FL_SEL = 11
NFL = 27
EXW = 1024 + 120 + 88


def build_program(n_stages=NSTAGE, self_sync=("dve",), dbg=False, trunc=99, nl=DEPTH, ncores=8, silu=True, nbanks=5, wsplit=1):
    nc = bass.Bass("TRN2", target_bir_lowering=False)
    dt = nc.dram_tensor
    xg = dt("xg", [2, G, D], F32, kind="ExternalInput").ap()
    cvec = dt("cvec", [128, 16], F32, kind="ExternalInput").ap()
    ada_w = dt("ada_w", [nl, D, 6 * D], F32, kind="ExternalInput").ap()
    w_in = dt("w_in", [nl, D, WIN_COLS], F32, kind="ExternalInput").ap()
    w_out = dt("w_out", [nl, D, D], F32, kind="ExternalInput").ap()
    w_up = dt("w_up", [nl, D, 2 * DFF], F32, kind="ExternalInput").ap()
    w_down = dt("w_down", [nl, DFF, D], F32, kind="ExternalInput").ap()
    pool_w = dt("pool_w", [nl, 4, 256, 256], F32, kind="ExternalInput").ap()
    vecs = dt("vecs", [nl, 128, NV], F32, kind="ExternalInput").ap()
    hlb = dt("hlb", [128, 4, 8], F32, kind="ExternalInput").ap()
    fng = dt("fng", [128, 16], F32, kind="ExternalInput").ap()
    flags = dt("flags", [128, NFL], F32, kind="ExternalInput").ap()
    invc = dt("invc", [128, 2, 4, 16], F32, kind="ExternalInput").ap()
    maskab = dt("maskab", [128, 2, 512], U8, kind="ExternalInput").ap()
    ident_d = dt("ident", [128, 128], F32, kind="ExternalInput").ap()
    tri_d = dt("tri", [64, 64], F32, kind="ExternalInput").ap()
    rmask_d = dt("rmask", [128, TT], F32, kind="ExternalInput").ap()
    outs = dt("outs", [4, G, D], F32, kind="ExternalOutput").ap()
    exs = [dt("exs%d" % s, [128, EXW], F32, kind="Internal").ap() for s in range(n_stages)]
    exr = [dt("exr%d" % s, [256, EXW], F32, kind="Internal").ap() for s in range(n_stages)]
    dbg_o = dt("dbg_o", [128, 16 * G], F32, kind="ExternalOutput").ap() if dbg else None

    st = ExitStack()
    with st:
        sb = lambda name, shape, dtp: st.enter_context(nc.sbuf_tensor("s_" + name, shape, dtp))
        xT = sb("xT", [128, KC, G], F32)
        hT = sb("hT", [128, KC, G], BF16)
        yT = sb("yT", [128, KC, G], BF16)
        wb = [sb("wb%d" % i, [128, 16 * 512], BF16) for i in range(2)]
        R = [sb("R%d" % i, [128, 528 if i < 6 else 512], F32) for i in range(11)]
        B = [sb("B%d" % i, [128, 512], BF16) for i in range(5)]
        vtok = sb("vtok", [64, 8, 128], BF16)
        khtok = sb("khtok", [64, 8, 128], BF16)
        pt = sb("pt", [64, 64], BF16)
        Sst = sb("Sst", [128, NH, 128], F32)
        Sbf = sb("Sbf", [128, 128], BF16)
        ptail = sb("ptail", [128, 8, 15], F32)
        ctail = sb("ctail", [128, NFC, 2], F32)
        modT = sb("modT", [128, 4, 96], F32)
        vec = sb("vec", [128, NV], F32)
        der = sb("der", [128, 64], F32)
        lbn = sb("lbn", [128, 4, 8], F32)
        lbt = sb("lbt", [128, 4, 8], F32)
        fngs = sb("fngs", [128, 16], F32)
        flg = sb("flg", [128, NFL], F32)
        invcs = sb("invcs", [128, 2, 4, 16], F32)
        mab = sb("mab", [128, 2, 512], U8)
        identf = sb("identf", [128, 128], F32)
        identb = sb("identb", [128, 128], BF16)
        onesf = sb("onesf", [128, 128], F32)
        tri = sb("tri", [64, 64], F32)
        rmask = sb("rmask", [128, TT], BF16)
        cact = sb("cact", [128, 16], BF16)
        cf = sb("cf", [128, 16], F32)
        pw = sb("pw", [128, 2, 2, 256], BF16)
        ps = lambda name, shape, dtp: st.enter_context(nc.psum_tensor("p_" + name, shape, dtp))
        pj = [ps("pj%d" % i, [128, 512], F32) for i in range(3)]
        pp = ps("pp", [128, 512], F32)
        opb = ps("opb", [128, 512], F32)
        upb = ps("upb", [128, 512], F32)
        tpb = ps("tpb", [128, 1024], BF16)
        nsb = ps("nsb", [128, 512], F32)

        P = Prog(nc, self_sync=self_sync)
        cnt = {"w": 0, "pj": 0}

        def wload(src_ap, ncols_total):
            i = cnt["w"] % 2
            cnt["w"] += 1
            dst = wb[i][:, 0:ncols_total]
            nk = src_ap.shape[0] // 128
            ncol = src_ap.shape[1]
            P.dma("pool", lambda e: e.dma_start(out=dst.rearrange("p (k n) -> p k n", n=ncol),
                                                in_=src_ap.rearrange("(k p) n -> p k n", p=128)),
                  reads=[], writes=["wb%d" % i], key="wb%d" % i)
            return i

        cnt["w3"] = 0
        W2P = [yT[:, 11 + j, :] for j in range(5)] + [R[6 + j][:, 0:512].bitcast(BF16) for j in range(3)]
        W2K = ["wb2"] + [("yT", 11 + j, t) for j in range(5) for t in range(NT)] + ["R6", "R7", "R8"]

        def wload3(src_ap):
            i = cnt["w3"] % 3
            cnt["w3"] += 1
            nk = src_ap.shape[0] // 128
            ncol = src_ap.shape[1]
            sv = src_ap.rearrange("(k p) n -> p k n", p=128)
            if i < 2:
                dst = wb[i][:, 0:nk * ncol].rearrange("p (k n) -> p k n", n=ncol)
                P.dma("pool", lambda e: e.dma_start(out=dst, in_=sv), reads=[], writes=["wb%d" % i], key="wb%d" % i)
                return ["wb%d" % i], (lambda k: dst[:, k, :])
            for j in range((nk + 1) // 2):
                kk = min(2, nk - 2 * j)
                d = W2P[j].rearrange("p (k n) -> p k n", n=512)[:, 0:kk, 0:ncol]
                P.dma("pool", lambda e, d=d, j=j, kk=kk: e.dma_start(out=d, in_=sv[:, 2 * j:2 * j + kk, :]),
                      reads=[], writes=W2K, key="wb2", grp=("w3", cnt["w3"]))
            return W2K, (lambda k: W2P[k // 2][:, (k % 2) * 512:(k % 2) * 512 + ncol])

        BK = [(pj[0], "pj0"), (pj[1], "pj1"), (pj[2], "pj2"), (opb, "opb"), (upb, "upb")]
        cnt["nb"] = nbanks

        def next_pj():
            i = cnt["pj"] % cnt["nb"]
            cnt["pj"] += 1
            return i

        def mm(out, lhsT, rhs, start, stop, reads, writes):
            P.op("pe", lambda e: e.matmul(out, lhsT=lhsT, rhs=rhs, start=start, stop=stop), reads, writes)

        def proj(wslot, wcol0, tile, dstbank):
            wv = wb[wslot][:].rearrange("p (k n) -> p k n", n=512)
            for k in range(KC):
                mm(BK[dstbank][0][:, :], wv[:, k, wcol0:wcol0 + 128], hT[:, k, tile * TT:(tile + 1) * TT],
                   k == 0, k == KC - 1, ["wb%d" % wslot, ("hT", tile)], [BK[dstbank][1]])

        def act(out, in_, func, reads, writes, bias=None, scale=None):
            kw = {}
            if bias is not None:
                kw["bias"] = bias
            if scale is not None:
                kw["scale"] = scale
            P.op("act", lambda e: e.activation(out=out, in_=in_, func=func, **kw), reads, writes)

        def ts(eng, out, in0, s1, s2, op0, op1, reads, writes):
            P.op(eng, lambda e: e.tensor_scalar(out=out, in0=in0, scalar1=s1, scalar2=s2, op0=op0, op1=op1), reads, writes)

        def stt(eng, out, in0, scalar, in1, op0, op1, reads, writes):
            P.op(eng, lambda e: e.scalar_tensor_tensor(out=out, in0=in0, scalar=scalar, in1=in1, op0=op0, op1=op1), reads, writes)

        def tt(eng, out, in0, in1, op, reads, writes):
            P.op(eng, lambda e: e.tensor_tensor(out=out, in0=in0, in1=in1, op=op), reads, writes)

        def cp(eng, out, in_, reads, writes):
            if eng == "act":
                P.op("act", lambda e: e.activation(out=out, in_=in_, func=AF.Copy), reads, writes)
            else:
                P.op(eng, lambda e: e.tensor_copy(out=out, in_=in_), reads, writes)

        def dcol(j):
            return der[:, j:j + 1]

        lbc0 = sb("lbc0", [128, 8], F32)
        lbc1 = sb("lbc1", [128, 8], F32)
        lbc1n = sb("lbc1n", [128, 8], F32)
        hngh = sb("hngh", [128, 1], F32)

        ld = lambda dst, src, key: P.dma("sp", lambda e: e.dma_start(out=dst, in_=src), reads=[], writes=[key], key="ld_" + key)
        ld(flg[:], flags, "flg")
        ld(invcs[:], invc, "invcs")
        ld(mab[:], maskab, "mab")
        ld(identf[:], ident_d, "identf")
        ld(tri[:], tri_d, "tri")
        P.dma("pool", lambda e: e.dma_start(out=rmask[:], in_=rmask_d), reads=[], writes=["rmask"], key="ld_rmask")
        ld(cf[:], cvec, "cf")
        ld(fngs[:], fng, "fngs")
        ld(lbn[:], hlb, "lbn")
        cp("dve", identb[:], identf[:], ["identf"], ["identb"])
        P.op("dve", lambda e: e.memset(onesf[:], 1.0), [], ["onesf"])
        P.op("dve", lambda e: e.memset(pp[:, :], 0.0), [], ["pp"])
        act(R[0][:, 0:16], cf[:], AF.Tanh, ["cf"], ["R0"], scale=0.5)
        stt("dve", R[0][:, 0:16], R[0][:, 0:16], 1.0, cf[:], ALU.add, ALU.mult, ["R0", "cf"], ["R0"])
        ts("dve", cact[:], R[0][:, 0:16], 0.5, None, ALU.mult, ALU.bypass, ["R0"], ["cact"])
        act(lbn[:], lbn[:], AF.Exp, ["lbn"], ["lbn"])
        tt("dve", R[1][:, 0:8], lbn[:, 0, :], lbn[:, 1, :], ALU.add, ["lbn"], ["R1"])
        tt("dve", R[1][:, 0:8], R[1][:, 0:8], lbn[:, 2, :], ALU.add, ["lbn", "R1"], ["R1"])
        tt("dve", R[1][:, 0:8], R[1][:, 0:8], lbn[:, 3, :], ALU.add, ["lbn", "R1"], ["R1"])
        P.op("dve", lambda e: e.reciprocal(out=R[1][:, 8:16], in_=R[1][:, 0:8]), ["R1"], ["R1"])
        P.op("dve", lambda e: e.memset(lbt[:, 0, :], 0.0), [], ["lbt"])
        tt("dve", lbt[:, 1, :], lbn[:, 1, :], R[1][:, 8:16], ALU.mult, ["lbn", "R1", "lbt"], ["lbt"])
        for l in (2, 3):
            tt("dve", R[1][:, 16:24], lbn[:, l, :], R[1][:, 8:16], ALU.mult, ["lbn", "R1"], ["R1"])
            tt("dve", lbt[:, l, :], lbt[:, l - 1, :], R[1][:, 16:24], ALU.add, ["lbt", "R1"], ["lbt"])
        ts("dve", lbt[:], lbt[:], 1.0, 0.0, ALU.min, ALU.max, ["lbt"], ["lbt"])

        def load_x(grp, pred):
            for tb in range(G // 128):
                for c4 in range(4):
                    stg = R[(tb * 4 + c4) % 2]
                    sk = "R%d" % ((tb * 4 + c4) % 2)
                    src = xg[grp, tb * 128:(tb + 1) * 128, c4 * 512:(c4 + 1) * 512]
                    P.dma("sp", lambda e, stg=stg, src=src: e.dma_start(out=stg[:, 0:512], in_=src), reads=[], writes=[sk], key="ldx_" + sk)
                    bank = next_pj()
                    for j in range(4):
                        P.op("pe", lambda e, j=j, stg=stg, bank=bank: e.transpose(BK[bank][0][:, j * 128:(j + 1) * 128], stg[:, j * 128:(j + 1) * 128], identf[:]),
                             [sk, "identf"], [BK[bank][1]])
                    dst = xT[:, c4 * 4:(c4 + 1) * 4, tb * 128:(tb + 1) * 128]
                    srcp = BK[bank][0][:, :].rearrange("p (j f) -> p j f", f=128)
                    keys = [("xT", c, tb // 4) for c in range(c4 * 4, c4 * 4 + 4)]
                    if pred is None:
                        cp("dve", dst, srcp, [BK[bank][1]], keys)
                    else:
                        m = mab[:, pred, :].rearrange("p (j f) -> p j f", f=128)
                        P.op("dve", lambda e, dst=dst, m=m, srcp=srcp: e.copy_predicated(out=dst, mask=m, data=srcp),
                             [BK[bank][1], "mab"] + keys, keys)

        def norm_to_h(gcol0, shift_ap_fn):
            for tile in range(NT):
                tsl = slice(tile * TT, (tile + 1) * TT)
                for c in range(KC):
                    sq = R[8 + c % 2]
                    sqk = "R%d" % (8 + c % 2)
                    act(sq[:, 0:TT], xT[:, c, tsl], AF.Square, [("xT", c, tile)], [sqk])
                    mm(nsb[:, :], onesf[:], sq[:, 0:TT], c == 0, c == KC - 1, ["onesf", sqk], ["nsb"])
                act(R[10][:, 0:TT], nsb[:, :], AF.Ln, ["nsb"], ["R10"], bias=EPS, scale=1.0 / D)
                act(R[10][:, 0:TT], R[10][:, 0:TT], AF.Exp, ["R10"], ["R10"], scale=-0.5)
                for c in range(KC):
                    tmp = R[6 + c % 2]
                    tk = "R%d" % (6 + c % 2)
                    stt("dve", tmp[:, 0:TT], xT[:, c, tsl], dcol(gcol0 + c), R[10][:, 0:TT], ALU.mult, ALU.mult,
                        [("xT", c, tile), "der", "R10"], [tk])
                    act(hT[:, c, tsl], tmp[:, 0:TT], AF.Identity, [tk, "modT"], [("hT", tile)], bias=shift_ap_fn(c), scale=1.0)

        def final_store(slot):
            for tile in range(NT):
                tsl = slice(tile * TT, (tile + 1) * TT)
                for c in range(KC):
                    sq = R[8 + c % 2]
                    sqk = "R%d" % (8 + c % 2)
                    act(sq[:, 0:TT], xT[:, c, tsl], AF.Square, [("xT", c, tile)], [sqk])
                    mm(nsb[:, :], onesf[:], sq[:, 0:TT], c == 0, c == KC - 1, ["onesf", sqk], ["nsb"])
                act(R[10][:, 0:TT], nsb[:, :], AF.Ln, ["nsb"], ["R10"], bias=EPS, scale=1.0 / D)
                act(R[10][:, 0:TT], R[10][:, 0:TT], AF.Exp, ["R10"], ["R10"], scale=-0.5)
                for c4 in range(4):
                    for cc in range(4):
                        c = c4 * 4 + cc
                        stt("dve", R[cc][:, 0:TT], xT[:, c, tsl], fngs[:, c:c + 1], R[10][:, 0:TT], ALU.mult, ALU.mult,
                            [("xT", c, tile), "fngs", "R10"], ["R%d" % cc])
                    for j in range(4):
                        bank = next_pj()
                        for cc in range(4):
                            P.op("pe", lambda e, j=j, cc=cc, bank=bank: e.transpose(BK[bank][0][:, cc * 128:(cc + 1) * 128], R[cc][:, j * 128:(j + 1) * 128], identf[:]),
                                 ["R%d" % cc, "identf"], [BK[bank][1]])
                        og = R[4 + j % 2]
                        ogk = "R%d" % (4 + j % 2)
                        cp("act", og[:, 0:512], BK[bank][0][:, :], [BK[bank][1]], [ogk])
                        t0 = tile * TT + j * 128
                        dst = outs[slot, t0:t0 + 128, c4 * 512:(c4 + 1) * 512]
                        P.dma("sp", lambda e, og=og, dst=dst: e.dma_start(out=dst, in_=og[:, 0:512]), reads=[ogk], writes=[("outs", slot, tile, c4, j)], key="st_" + ogk)

        def stage(s):
            slot = s % 4
            P.epoch()
            P.dma("sp", lambda e: e.dma_start(out=vec[:], in_=vecs[slot]), reads=[], writes=["vec"], key="ld_vec")
            if s < 4:
                for blk in range(24):
                    wslot = wload(ada_w[slot, :, blk * 512:(blk + 1) * 512], 16 * 512)
                    wv = wb[wslot][:].rearrange("p (k n) -> p k n", n=512)
                    for j4 in range(4):
                        j = blk * 4 + j4
                        for k in range(KC):
                            mm(nsb[:, j:j + 1], wv[:, k, j4 * 128:(j4 + 1) * 128], cact[:, k:k + 1], k == 0, k == KC - 1,
                               ["wb%d" % wslot, "cact"], ["nsb"])
                tt("dve", modT[:, slot, :], nsb[:, 0:96], vec[:, V_ADAB:V_ADAB + 96], ALU.add, ["nsb", "vec"], ["modT"])
            md = lambda i, c: modT[:, slot, i * 16 + c:i * 16 + c + 1]
            stt("dve", der[:, 0:16], modT[:, slot, 16:32], 1.0, vec[:, V_MIXG:V_MIXG + 16], ALU.add, ALU.mult, ["modT", "vec"], ["der"])
            stt("dve", der[:, 16:32], modT[:, slot, 64:80], 1.0, vec[:, V_FFNG:V_FFNG + 16], ALU.add, ALU.mult, ["modT", "vec", "der"], ["der"])
            ts("dve", der[:, 32:48], modT[:, slot, 32:48], flg[:, FL_ACT + s:FL_ACT + s + 1], None, ALU.mult, ALU.bypass, ["modT", "flg", "der"], ["der"])
            ts("dve", der[:, 48:64], modT[:, slot, 80:96], flg[:, FL_ACT + s:FL_ACT + s + 1], (1.0 if silu else 0.5), ALU.mult, ALU.mult, ["modT", "flg", "der"], ["der"])
            ts("dve", lbc0[:], lbt[:, 0, :], flg[:, FL_SEL + slot * 4:FL_SEL + slot * 4 + 1], None, ALU.mult, ALU.bypass, ["lbt", "flg"], ["lbc0"])
            for l in range(1, 4):
                stt("dve", lbc0[:], lbt[:, l, :], flg[:, FL_SEL + slot * 4 + l:FL_SEL + slot * 4 + l + 1], lbc0[:], ALU.mult, ALU.add,
                    ["lbt", "flg", "lbc0"], ["lbc0"])
            ts("dve", lbc1[:], lbc0[:], -0.5, 0.5, ALU.mult, ALU.add, ["lbc0"], ["lbc1"])
            ts("dve", lbc1n[:], lbc1[:], -1.0, None, ALU.mult, ALU.bypass, ["lbc1"], ["lbc1n"])
            ts("dve", lbc0[:], lbc0[:], 0.5, 0.5, ALU.mult, ALU.add, ["lbc0", "lbc1"], ["lbc0"])
            ts("dve", hngh[:], vec[:, V_HNG:V_HNG + 1], 0.5, None, ALU.mult, ALU.bypass, ["vec"], ["hngh"])

            if s == 0:
                P.op("pool", lambda e: e.memset(Sst[:], 0.0), [], ["Sst"])
                P.op("pool", lambda e: e.memset(ptail[:], 0.0), [], ["ptail"])
                P.op("pool", lambda e: e.memset(ctail[:], 0.0), [], ["ctail"])
            else:
                Sfl = Sst[:].rearrange("p h v -> p (h v)")
                ptf = ptail[:].rearrange("p c t -> p (c t)")
                ctf = ctail[:].rearrange("p c t -> p (c t)")
                src = exr[s - 1]
                P.dma("sp", lambda e: e.dma_start(out=Sfl, in_=src[0:128, 0:1024]), reads=["exr%d" % (s - 1)], writes=["Sst"], key="ld_S")
                P.dma("sp", lambda e: e.dma_start(out=ptf, in_=src[0:128, 1024:1144]), reads=["exr%d" % (s - 1)], writes=["ptail"], key="ld_pt")
                P.dma("sp", lambda e: e.dma_start(out=ctf, in_=src[0:128, 1144:1232]), reads=["exr%d" % (s - 1)], writes=["ctail"], key="ld_ct")
                cb = flg[:, FL_CB:FL_CB + 1]
                ts("dve", Sfl, Sfl, cb, None, ALU.mult, ALU.bypass, ["Sst", "flg"], ["Sst"])
                ts("dve", ptf, ptf, cb, None, ALU.mult, ALU.bypass, ["ptail", "flg"], ["ptail"])
                ts("dve", ctf, ctf, cb, None, ALU.mult, ALU.bypass, ["ctail", "flg"], ["ctail"])
                if s >= 4:
                    src2 = exr[s - 3]
                    ca = flg[:, FL_CA:FL_CA + 1]
                    P.dma("sp", lambda e: e.dma_start(out=R[0][:, 0:512], in_=src2[128:256, 0:512]), reads=["exr%d" % (s - 3)], writes=["R0"], key="ld_R0")
                    P.dma("sp", lambda e: e.dma_start(out=R[1][:, 0:512], in_=src2[128:256, 512:1024]), reads=["exr%d" % (s - 3)], writes=["R1"], key="ld_R1")
                    P.dma("sp", lambda e: e.dma_start(out=R[2][:, 0:208], in_=src2[128:256, 1024:1232]), reads=["exr%d" % (s - 3)], writes=["R2"], key="ld_R2")
                    stt("dve", Sfl[:, 0:512], R[0][:, 0:512], ca, Sfl[:, 0:512], ALU.mult, ALU.add, ["R0", "flg", "Sst"], ["Sst"])
                    stt("dve", Sfl[:, 512:1024], R[1][:, 0:512], ca, Sfl[:, 512:1024], ALU.mult, ALU.add, ["R1", "flg", "Sst"], ["Sst"])
                    stt("dve", ptf, R[2][:, 0:120], ca, ptf, ALU.mult, ALU.add, ["R2", "flg", "ptail"], ["ptail"])
                    stt("dve", ctf, R[2][:, 120:208], ca, ctf, ALU.mult, ALU.add, ["R2", "flg", "ctail"], ["ctail"])

            norm_to_h(0, lambda c: md(0, c))

            if trunc <= 1:
                return
            QSC = 0.5 * (128.0 ** -0.5)
            cnt["nb"] = 3
            for h in range(NH):
                wslot = wload(w_in[slot, :, h * 512:(h + 1) * 512], 16 * 512)
                for tile in range(NT):
                    tsl = slice(tile * TT, (tile + 1) * TT)
                    c0 = lbc0[:, h:h + 1]; c1 = lbc1[:, h:h + 1]; c1n = lbc1n[:, h:h + 1]
                    bz = next_pj(); proj(wslot, 128, tile, bz)
                    act(R[0][:, 0:TT], BK[bz][0][:, :], AF.Tanh, [BK[bz][1]], ["R0"], scale=0.5)
                    bq = next_pj(); proj(wslot, 0, tile, bq)
                    act(R[4][:, 0:TT], BK[bq][0][:, :], AF.Tanh, [BK[bq][1]], ["R4"], scale=0.5)
                    stt("dve", R[4][:, 0:TT], R[4][:, 0:TT], 1.0, BK[bq][0][:, :], ALU.add, ALU.mult, ["R4", BK[bq][1]], ["R4"])
                    bg = next_pj(); proj(wslot, 384, tile, bg)
                    act(R[5][:, 0:TT], BK[bg][0][:, :], AF.Tanh, [BK[bg][1]], ["R5"], scale=0.5)
                    stt("dve", R[5][:, 0:TT], R[5][:, 0:TT], 1.0, BK[bg][0][:, :], ALU.add, ALU.mult, ["R5", BK[bg][1]], ["R5"])
                    bi = next_pj(); proj(wslot, 256, tile, bi)
                    cp("act", B[4][:, 0:TT], BK[bi][0][:, :], [BK[bi][1]], ["B4"])
                    for c in range(8):
                        P.op("pe", lambda e, c=c: e.transpose(tpb[0:64, c * 128:(c + 1) * 128], B[4][:, c * 64:(c + 1) * 64], identb[:]),
                             ["B4", "identb"], ["tpb"])
                    cp("act", vtok[:].rearrange("p c v -> p (c v)"), tpb[0:64, :], ["tpb"], ["vtok"])
                    ts("dve", R[1][:, 0:TT], R[0][:, 0:TT], c1, c0, ALU.mult, ALU.add, ["R0", "lbc0", "lbc1"], ["R1"])
                    ts("dve", R[2][:, 0:TT], R[0][:, 0:TT], c1n, c1, ALU.mult, ALU.add, ["R0", "lbc1", "lbc1n"], ["R2"])
                    act(R[0][:, 0:TT], R[1][:, 0:TT], AF.Ln, ["R1"], ["R0"])
                    P.op("dve", lambda e: e.tensor_tensor_scan(out=R[3][:, 0:TT], data0=rmask[:], data1=R[0][:, 0:TT], initial=0.0, op0=ALU.mult, op1=ALU.add),
                         ["rmask", "R0"], ["R3"])
                    b32 = R[3][:, 0:TT].rearrange("p (n j) -> p n j", j=32)
                    b64 = R[3][:, 0:TT].rearrange("p (n j) -> p n j", j=64)
                    v32 = lambda r: r[:, 0:TT].rearrange("p (n j) -> p n j", j=32)
                    v64 = lambda r: r[:, 0:TT].rearrange("p (n j) -> p n j", j=64)
                    vxy = lambda r: r[:, 0:TT].rearrange("p (n b j) -> p n b j", b=2, j=32)
                    tt("dve", v32(R[6]), b32, b32[:, :, 15:16].to_broadcast([128, 16, 32]), ALU.subtract, ["R3"], ["R6"])
                    act(R[8][:, 0:TT], R[6][:, 0:TT], AF.Exp, ["R6"], ["R8"])
                    stt("dve", B[1][:, 0:TT], R[4][:, 0:TT], QSC, R[8][:, 0:TT], ALU.mult, ALU.mult, ["R4", "R8"], ["B1"])
                    act(R[9][:, 0:TT], R[6][:, 0:TT], AF.Exp, ["R6"], ["R9"], scale=-1.0)
                    tt("dve", B[0][:, 0:TT], R[2][:, 0:TT], R[9][:, 0:TT], ALU.mult, ["R2", "R9"], ["B0"])
                    tt("dve", v64(R[7]), b64, b64[:, :, 31:32].to_broadcast([128, 8, 64]), ALU.subtract, ["R3"], ["R7"])
                    act(vxy(R[8])[:, :, 1, :], vxy(R[7])[:, :, 1, :], AF.Exp, ["R7", "B1"], ["R8"])
                    act(vxy(R[8])[:, :, 0, :], vxy(R[7])[:, :, 0, :], AF.Exp, ["R7"], ["R8"], scale=-1.0)
                    stt("dve", B[3][:, 0:256].rearrange("p (n j) -> p n j", j=32), vxy(R[4])[:, :, 1, :], QSC, vxy(R[8])[:, :, 1, :], ALU.mult, ALU.mult,
                        ["R4", "R8"], ["B3"])
                    tt("dve", B[3][:, 256:512].rearrange("p (n j) -> p n j", j=32), vxy(R[2])[:, :, 0, :], vxy(R[8])[:, :, 0, :], ALU.mult,
                       ["R2", "R8", "B3"], ["B3"])
                    act(R[10][:, 0:TT], R[3][:, 0:TT], AF.Exp, ["R3"], ["R10"])
                    stt("dve", B[2][:, 0:TT], R[4][:, 0:TT], QSC, R[10][:, 0:TT], ALU.mult, ALU.mult, ["R4", "R10"], ["B2"])
                    tt("dve", v64(R[6]), b64, b64[:, :, 63:64].to_broadcast([128, 8, 64]), ALU.subtract, ["R3", "R6"], ["R6"])
                    act(R[9][:, 0:TT], R[6][:, 0:TT], AF.Exp, ["R6", "B0"], ["R9"], scale=-1.0)
                    tt("dve", B[4][:, 0:TT], R[2][:, 0:TT], R[9][:, 0:TT], ALU.mult, ["R2", "R9"], ["B4"])
                    for c in range(8):
                        P.op("pe", lambda e, c=c: e.transpose(tpb[0:64, c * 128:(c + 1) * 128], B[4][:, c * 64:(c + 1) * 64], identb[:]),
                             ["B4", "identb", "vtok"], ["tpb"])
                    cp("act", khtok[:].rearrange("p c v -> p (c v)"), tpb[0:64, :], ["tpb"], ["khtok"])
                    kt32 = B[0][:, 0:TT].rearrange("p (n b j) -> p n b j", b=2, j=32)
                    qt32 = B[1][:, 0:TT].rearrange("p (n b j) -> p n b j", b=2, j=32)
                    qo32 = B[3][:, 0:256].rearrange("p (n j) -> p n j", j=32)
                    ko32 = B[3][:, 256:512].rearrange("p (n j) -> p n j", j=32)
                    if tile == 0:
                        cp("act", Sbf[:], Sst[:, h, :], ["Sst"], ["Sbf"])
                    for c in range(8):
                        mm(pp[0:32, 0:32], kt32[:, c, 0, :], qt32[:, c, 0, :], True, True, ["B0", "B1", "pt"], ["pp"])
                        mm(pp[0:32, 32:64], ko32[:, c, :], qo32[:, c, :], True, True, ["B3"], ["pp"])
                        mm(pp[32:64, 32:64], kt32[:, c, 1, :], qt32[:, c, 1, :], True, True, ["B0", "B1"], ["pp"])
                        tt("dve", pt[:], pp[0:64, 0:64], tri[:], ALU.mult, ["pp", "tri"], ["pt"])
                        mm(opb[:, c * 64:(c + 1) * 64], vtok[:, c, :], pt[:], True, False, ["vtok", "pt"], ["opb"])
                        mm(opb[:, c * 64:(c + 1) * 64], Sbf[:], B[2][:, c * 64:(c + 1) * 64], False, True, ["Sbf", "B2"], ["opb"])
                        mm(upb[:, 0:128], khtok[:, c, :], vtok[:, c, :], True, True, ["khtok", "vtok"], ["upb"])
                        stt("dve", Sbf[:], Sst[:, h, :], R[10][:, c * 64 + 63:c * 64 + 64], upb[:, 0:128], ALU.mult, ALU.add,
                            ["Sst", "R10", "upb"], ["Sbf"])
                        stt("dve", Sst[:, h, :], Sst[:, h, :], R[10][:, c * 64 + 63:c * 64 + 64], upb[:, 0:128], ALU.mult, ALU.add,
                            ["Sst", "R10", "upb"], ["Sst"])
                    act(R[8][:, 0:TT], opb[:, :], AF.Square, ["opb"], ["R8"])
                    mm(nsb[:, :], onesf[:], R[8][:, 0:TT], True, True, ["onesf", "R8"], ["nsb"])
                    act(R[9][:, 0:TT], nsb[:, :], AF.Ln, ["nsb"], ["R9"], bias=EPS, scale=1.0 / 128.0)
                    act(R[9][:, 0:TT], R[9][:, 0:TT], AF.Exp, ["R9"], ["R9"], scale=-0.5)
                    stt("dve", R[6][:, 0:TT], opb[:, :], hngh[:, 0:1], R[9][:, 0:TT], ALU.mult, ALU.mult, ["opb", "hngh", "R9", "R6"], ["R6"])
                    tt("dve", yT[:, h, tsl], R[6][:, 0:TT], R[5][:, 0:TT], ALU.mult, ["R6", "R5"], [("yT", h, tile)])

            if trunc <= 2:
                return
            cnt["nb"] = nbanks
            for pb in range(2):
                wslot = wload(w_in[slot, :, 4096 + pb * 512:4096 + (pb + 1) * 512], 16 * 512)
                P.dma("pool", lambda e, pb=pb: e.dma_start(out=pw[:], in_=pool_w[slot, pb * 2:(pb + 1) * 2].rearrange("g (dc p) e -> p g dc e", p=128)),
                      reads=[], writes=["pw"], key="ld_pw")
                for gl in range(2):
                    g = pb * 2 + gl
                    nlev = g + 1
                    invw = 1.0 / (2 ** nlev)
                    for tile in range(NT):
                        tsl = slice(tile * TT, (tile + 1) * TT)
                        for dc in range(2):
                            ch = g * 2 + dc
                            bu = next_pj(); proj(wslot, (gl * 2 + dc) * 128, tile, bu)
                            U = R[dc]; Uk = "R%d" % dc
                            cp("act", U[:, 0:15], ptail[:, ch, :], ["ptail"], [Uk])
                            cp("act", U[:, 15:15 + TT], BK[bu][0][:, :], [BK[bu][1], Uk], [Uk])
                            cp("act", ptail[:, ch, :], U[:, TT:TT + 15], [Uk, "ptail"], ["ptail"])
                            prev = U; prevk = Uk
                            for lev in range(nlev):
                                sh = 2 ** lev
                                nx = R[2 + dc * 2 + lev % 2]; nxk = "R%d" % (2 + dc * 2 + lev % 2)
                                lo = 2 ** (lev + 1) - 1
                                tt("dve", nx[:, lo:15 + TT], prev[:, lo:15 + TT], prev[:, lo - sh:15 + TT - sh], ALU.add,
                                   [prevk, nxk], [nxk])
                                prev = nx; prevk = nxk
                            if tile == 0:
                                iv = invcs[:, 0 if s < 4 else 1, g, :]
                                tt("dve", R[6 + dc][:, 0:16], prev[:, 15:31], iv, ALU.mult, [prevk, "invcs", "R%d" % (6 + dc)], ["R%d" % (6 + dc)])
                                tt("dve", B[dc][:, 0:16], R[6 + dc][:, 0:16], U[:, 15:31], ALU.subtract, ["R%d" % (6 + dc), Uk], ["B%d" % dc])
                                stt("dve", B[dc][:, 16:TT], prev[:, 31:15 + TT], invw, U[:, 31:15 + TT], ALU.mult, ALU.subtract, [prevk, Uk, "B%d" % dc], ["B%d" % dc])
                            else:
                                stt("dve", B[dc][:, 0:TT], prev[:, 15:15 + TT], invw, U[:, 15:15 + TT], ALU.mult, ALU.subtract, [prevk, Uk], ["B%d" % dc])
                        for ec in range(2):
                            bo = next_pj()
                            for dc in range(2):
                                mm(BK[bo][0][:, :], pw[:, gl, dc, ec * 128:(ec + 1) * 128], B[dc][:, 0:TT], dc == 0, dc == 1, ["pw", "B%d" % dc], [BK[bo][1]])
                            ych = 8 + g * 2 + ec
                            act(yT[:, ych, tsl], BK[bo][0][:, :], AF.Identity, [BK[bo][1], "vec"], [("yT", ych, tile)],
                                scale=vec[:, V_PSC + g * 2 + ec:V_PSC + g * 2 + ec + 1])

            if trunc <= 3:
                return
            for ob in range(4):
                wslot = wload(w_out[slot, :, ob * 512:(ob + 1) * 512], 16 * 512)
                wv = wb[wslot][:].rearrange("p (k n) -> p k n", n=512)
                for oc in range(4):
                    c = ob * 4 + oc
                    for tile in range(NT):
                        tsl = slice(tile * TT, (tile + 1) * TT)
                        bo = next_pj()
                        for k in range(KC):
                            mm(BK[bo][0][:, :], wv[:, k, oc * 128:(oc + 1) * 128], yT[:, k, tsl], k == 0, k == KC - 1,
                               ["wb%d" % wslot, ("yT", k, tile)], [BK[bo][1]])
                        stt("dve", xT[:, c, tsl], BK[bo][0][:, :], dcol(32 + c), xT[:, c, tsl], ALU.mult, ALU.add,
                            [BK[bo][1], "der", ("xT", c, tile)], [("xT", c, tile)])

            if trunc <= 4:
                return
            norm_to_h(16, lambda c: md(3, c))

            if trunc <= 5:
                return
            for q in range(NQ):
                for jb in range(FQ // 2 + 1):
                    fcs = [q * FQ + jb * 2 + i for i in range(2) if jb * 2 + i < FQ]
                    col0 = fcs[0] * 256
                    ncol = len(fcs) * 256
                    wkeys, kap = wload3(w_up[slot, :, col0:col0 + ncol])
                    for i, fc in enumerate(fcs):
                        fl = fc - q * FQ
                        for tile in range(NT):
                            tsl = slice(tile * TT, (tile + 1) * TT)
                            ba = next_pj()
                            for k in range(KC):
                                mm(BK[ba][0][:, :], kap(k)[:, i * 256:i * 256 + 128], hT[:, k, tsl], k == 0, k == KC - 1,
                                   wkeys + [("hT", tile)], [BK[ba][1]])
                            bv = next_pj()
                            for k in range(KC):
                                mm(BK[bv][0][:, :], kap(k)[:, i * 256 + 128:i * 256 + 256], hT[:, k, tsl], k == 0, k == KC - 1,
                                   wkeys + [("hT", tile)], [BK[bv][1]])
                            par = (fl * NT + tile) % 2
                            A = R[par]; Ak = "R%d" % par
                            C = R[2 + par]; Ck = "R%d" % (2 + par)
                            Tn = R[4 + par]; Tk = "R%d" % (4 + par)
                            cp("act", A[:, 0:2], ctail[:, fc, :], ["ctail"], [Ak])
                            cp("act", A[:, 2:2 + TT], BK[ba][0][:, :], [BK[ba][1], Ak], [Ak])
                            cp("act", ctail[:, fc, :], A[:, TT:TT + 2], [Ak, "ctail"], ["ctail"])
                            cw = lambda tap: vec[:, V_CW + tap * NFC + fc:V_CW + tap * NFC + fc + 1]
                            act(C[:, 0:TT], BK[ba][0][:, :], AF.Identity, [BK[ba][1], "vec"], [Ck], bias=vec[:, V_CB + fc:V_CB + fc + 1], scale=cw(2))
                            stt("dve", C[:, 0:TT], A[:, 1:1 + TT], cw(1), C[:, 0:TT], ALU.mult, ALU.add, [Ak, "vec", Ck], [Ck])
                            stt("dve", C[:, 0:TT], A[:, 0:TT], cw(0), C[:, 0:TT], ALU.mult, ALU.add, [Ak, "vec", Ck], [Ck])
                            if silu:
                                act(Tn[:, 0:TT], C[:, 0:TT], AF.Silu, [Ck], [Tk])
                            else:
                                act(Tn[:, 0:TT], C[:, 0:TT], AF.Tanh, [Ck], [Tk], scale=0.5)
                                stt("dve", Tn[:, 0:TT], Tn[:, 0:TT], 1.0, C[:, 0:TT], ALU.add, ALU.mult, [Tk, Ck], [Tk])
                            tt("dve", yT[:, fl, tsl], Tn[:, 0:TT], BK[bv][0][:, :], ALU.mult, [Tk, BK[bv][1]], [("yT", fl, tile)])
                for ob in range(4):
                    wkeys, kap = wload3(w_down[slot, q * FQ * 128:(q + 1) * FQ * 128, ob * 512:(ob + 1) * 512])
                    for oc in range(4):
                        c = ob * 4 + oc
                        for tile in range(NT):
                            tsl = slice(tile * TT, (tile + 1) * TT)
                            bo = next_pj()
                            for k in range(FQ):
                                mm(BK[bo][0][:, :], kap(k)[:, oc * 128:(oc + 1) * 128], yT[:, k, tsl], k == 0, k == FQ - 1,
                                   wkeys + [("yT", k, tile)], [BK[bo][1]])
                            stt("dve", xT[:, c, tsl], BK[bo][0][:, :], dcol(48 + c), xT[:, c, tsl], ALU.mult, ALU.add,
                                [BK[bo][1], "der", ("xT", c, tile)], [("xT", c, tile)])

            Sfl = Sst[:].rearrange("p h v -> p (h v)")
            P.dma("sp", lambda e: e.dma_start(out=exs[s][:, 0:1024], in_=Sfl), reads=["Sst"], writes=["exs%d" % s], key="exs_a")
            P.dma("sp", lambda e: e.dma_start(out=exs[s][:, 1024:1144], in_=ptail[:].rearrange("p c t -> p (c t)")), reads=["ptail"], writes=["exs%d" % s], key="exs_b")
            P.dma("sp", lambda e: e.dma_start(out=exs[s][:, 1144:1232], in_=ctail[:].rearrange("p c t -> p (c t)")), reads=["ctail"], writes=["exs%d" % s], key="exs_c")
            P.dma("pool", lambda e: e.collective_compute("AllGather", ALU.bypass, replica_groups=[[2 * i, 2 * i + 1] for i in range(ncores // 2)],
                                                          ins=[exs[s]], outs=[exr[s]]),
                  reads=["exs%d" % s], writes=["exr%d" % s], key="cc", inc=1)

        load_x(0, None)
        for s in range(n_stages):
            if s == 4:
                load_x(1, 0)
            if s == 5:
                load_x(1, 1)
            stage(s)
            if dbg and s == n_stages - 1:
                P.dma("sp", lambda e: e.dma_start(out=dbg_o, in_=xT[:].rearrange("p c t -> p (c t)")),
                      reads=[("xT", c, t) for c in range(KC) for t in range(NT)], writes=["dbg_o"], key="dbg")
            if s in (3, 4, 7, 8):
                final_store({3: 0, 4: 1, 7: 2, 8: 3}[s])
        P.op("sp", None, reads=[("outs", sl, tl, c4, j) for sl in range(4) for tl in range(NT) for c4 in range(4) for j in range(4)] + ["dbg_o"] + ["exr%d" % s for s in range(n_stages)])
        P.build(st)
    return nc


def _colmajor(v):
    return np.ascontiguousarray(np.asarray(v).reshape(-1, 128).T)


def prep_inputs(x, c, ada_w, ada_b, mix_norm_g, w_in, hgrn_lower_bounds, hgrn_norm_g, pool_w,
                pool_scale, w_out, ffn_norm_g, w_up, conv_w, conv_b, w_down, final_norm_g):
    f32 = np.float32
    x = np.asarray(x, f32); c = np.asarray(c, f32)
    perm_in = []
    for h in range(NH):
        for blk in range(4):
            perm_in.extend(range(blk * 1024 + h * 128, blk * 1024 + (h + 1) * 128))
    perm_in.extend(range(4096, 5120))
    perm_in = np.array(perm_in)
    perm_up = []
    for fc in range(NFC):
        perm_up.extend(range(fc * 128, (fc + 1) * 128))
        perm_up.extend(range(DFF + fc * 128, DFF + (fc + 1) * 128))
    perm_up = np.array(perm_up)
    w_in_p = np.ascontiguousarray(np.asarray(w_in, f32)[:, :, perm_in])
    w_up_p = np.ascontiguousarray(np.asarray(w_up, f32)[:, :, perm_up])
    vecs = np.zeros((DEPTH, 128, NV), f32)
    for l in range(DEPTH):
        vecs[l, :, V_ADAB:V_ADAB + 96] = _colmajor(ada_b[l])
        vecs[l, :, V_MIXG:V_MIXG + 16] = _colmajor(mix_norm_g[l])
        vecs[l, :, V_FFNG:V_FFNG + 16] = _colmajor(ffn_norm_g[l])
        vecs[l, :, V_HNG] = np.asarray(hgrn_norm_g[l])
        vecs[l, :, V_PSC:V_PSC + 8] = _colmajor(pool_scale[l])
        for tap in range(3):
            vecs[l, :, V_CW + tap * NFC:V_CW + (tap + 1) * NFC] = _colmajor(conv_w[l, tap])
        vecs[l, :, V_CB:V_CB + NFC] = _colmajor(conv_b[l])
    hlb = np.ascontiguousarray(np.asarray(hgrn_lower_bounds, f32).reshape(4, 8, 128).transpose(2, 0, 1))
    fng = _colmajor(final_norm_g).astype(f32)
    ident = np.eye(128, dtype=f32)
    tri = np.triu(np.ones((64, 64), f32))
    rmask = np.ones((128, TT), f32); rmask[:, ::64] = 0.0
    roll = lambda a: np.ascontiguousarray(np.roll(np.asarray(a, f32), 1, axis=0))
    shared = {
        0: dict(ada_w=np.asarray(ada_w, f32), w_in=w_in_p, w_out=np.asarray(w_out, f32), w_up=w_up_p,
                w_down=np.asarray(w_down, f32), pool_w=np.asarray(pool_w, f32), vecs=vecs),
    }
    shared[1] = {k: roll(v) for k, v in shared[0].items()}
    in_maps = []
    for core in range(8):
        b = core // 2; role = core % 2
        xg = np.ascontiguousarray(np.stack([x[b, (role + 0) * G:(role + 1) * G], x[b, (role + 2) * G:(role + 3) * G]]))
        flags = np.zeros((128, NFL), f32)
        flags[:, FL_CB] = 1.0 if role == 1 else 0.0
        flags[:, FL_CA] = 1.0 if role == 0 else 0.0
        for s in range(NSTAGE):
            active = (s <= 7) if role == 0 else (s >= 1)
            flags[:, FL_ACT + s] = 1.0 if active else 0.0
        for slot in range(4):
            layer = (slot - role) % 4
            flags[:, FL_SEL + slot * 4 + layer] = 1.0
        invc = np.zeros((128, 2, 4, 16), f32)
        for g in range(4):
            w = 2 ** (g + 1)
            invc[:, :, g, :] = 1.0 / w
            if role == 0:
                invc[:, 0, g, :] = 1.0 / np.minimum(np.arange(1, 17), w)
        maskab = np.zeros((128, 2, 512), np.uint8)
        maskab[:, role, :] = 1
        m = dict(shared[role])
        m.update(xg=xg, cvec=_colmajor(c[b]).astype(f32), hlb=hlb, fng=fng, flags=flags, invc=invc,
                 maskab=maskab, ident=ident, tri=tri, rmask=rmask)
        in_maps.append(m)
    return in_maps


def assemble(results):
    out = np.zeros((4, 4 * G, D), np.float32)
    for core in range(8):
        b = core // 2; role = core % 2
        o = results[core]["outs"]
        if role == 0:
            out[b, 0:G] = o[0]; out[b, 2 * G:3 * G] = o[2]
        else:
            out[b, G:2 * G] = o[1]; out[b, 3 * G:4 * G] = o[3]
    return out


_NC_CACHE = {}


def kernel(**inputs):
    in_maps = prep_inputs(**inputs)
    if "nc" not in _NC_CACHE:
        _NC_CACHE["nc"] = build_program()
    res = run_bass_kernel_spmd(_NC_CACHE["nc"], in_maps, core_ids=list(range(8)))
    return assemble(res.results)
```
